# Optimizing a Trainium2 kernel written in Bass

```python
import math
import jax
import jax.numpy as jnp
from jax import lax
import numpy as np

D_MODEL = 4096
BATCH = 1
SEQ = 16384
DEPTH = 2

HEAD_DIM = 128
A_WIDTH = D_MODEL // 2
A_HEADS = A_WIDTH // HEAD_DIM
B_WIDTH = D_MODEL - A_WIDTH
CONV_WIDTH = 3
DILATED_PATTERNS = ((128, 1), (512, 4), (2048, 16))

MLA_HEADS = D_MODEL // 128
Q_LORA = 1536
KV_LORA = 512
QK_NOPE = 128
QK_ROPE = 64
V_DIM = 128
MLA_IN = Q_LORA + KV_LORA + QK_ROPE

FFN_HIDDEN = -(-8 * D_MODEL // (3 * 256)) * 256

ROPE_THETA = 10000.0
Q_BLOCK = 128
LN_EPS = 1e-5
RMS_EPS = 1e-6
NEG = -1e30
ALPHA = (2.0 * DEPTH) ** 0.25
BETA = (8.0 * DEPTH) ** -0.25
N_EVEN = (DEPTH + 1) // 2
N_ODD = DEPTH // 2

kernel_name = "hybrid_dilated_conv_mla_deepnorm"


def layernorm(x, g, b):
    x32 = x.astype(jnp.float32)
    mu = jnp.mean(x32, axis=-1, keepdims=True)
    var = jnp.mean(jnp.square(x32 - mu), axis=-1, keepdims=True)
    return ((x32 - mu) * lax.rsqrt(var + LN_EPS) * g.astype(jnp.float32) + b.astype(jnp.float32)).astype(x.dtype)


def rmsnorm(x, g):
    x32 = x.astype(jnp.float32)
    r = lax.rsqrt(jnp.mean(jnp.square(x32), axis=-1, keepdims=True) + RMS_EPS)
    return (x32 * r * g.astype(jnp.float32)).astype(x.dtype)


def rope(x, pos):
    d = x.shape[-1]
    half = d // 2
    inv = ROPE_THETA ** (-jnp.arange(half, dtype=jnp.float32) * 2.0 / d)
    ang = pos.astype(jnp.float32)[:, None] * inv[None, :]
    cos = jnp.cos(ang)[:, None, :]
    sin = jnp.sin(ang)[:, None, :]
    x32 = x.astype(jnp.float32)
    x1, x2 = x32[..., :half], x32[..., half:]
    return jnp.concatenate([x1 * cos - x2 * sin, x2 * cos + x1 * sin], axis=-1).astype(x.dtype)


def dilated_branch(q, k, v, window, dilation):
    B, S, H, D = q.shape
    blk = window // dilation
    span = dilation * blk
    sp = -(-S // span) * span
    nb = sp // span

    def to_sub(t):
        t = jnp.pad(t, ((0, 0), (0, sp - S), (0, 0), (0, 0)))
        t = t.reshape(B, sp // dilation, dilation, H, D).swapaxes(1, 2)
        return t.reshape(B, dilation, nb, blk, H, D)

    def with_prev(t):
        prev = jnp.pad(t[:, :, :-1], ((0, 0), (0, 0), (1, 0), (0, 0), (0, 0), (0, 0)))
        return jnp.concatenate([prev, t], axis=3)

    qs = to_sub(q).astype(jnp.float32)
    kk = with_prev(to_sub(k)).astype(jnp.float32)
    vv = with_prev(to_sub(v)).astype(jnp.float32)
    s = jnp.einsum('brnqhd,brnkhd->brnhqk', qs, kk) * (D ** -0.5)
    qi = jnp.arange(blk)[:, None]
    ki = jnp.arange(2 * blk)[None, :]
    dist = qi + blk - ki
    band = (dist >= 0) & (dist <= blk)
    has_prev = (jnp.arange(nb) > 0)[:, None, None] | (ki >= blk)[None]
    valid = band[None] & has_prev
    s = jnp.where(valid[None, None, :, None], s, NEG)
    m = jnp.max(s, axis=-1, keepdims=True)
    p = jnp.exp(s - m)
    den = jnp.sum(p, axis=-1)
    o = jnp.einsum('brnhqk,brnkhd->brnqhd', p, vv) / den.swapaxes(-1, -2)[..., None]
    lse = (m[..., 0] + jnp.log(den)).swapaxes(-1, -2)

    def from_sub(t):
        t = t.reshape(B, dilation, sp // dilation, *t.shape[4:]).swapaxes(1, 2)
        return t.reshape(B, sp, *t.shape[3:])[:, :S]

    return from_sub(o), from_sub(lse)


def dilated_mixture(q, k, v):
    outs, lses = [], []
    for window, dilation in DILATED_PATTERNS:
        o, l = dilated_branch(q, k, v, window, dilation)
        outs.append(o)
        lses.append(l)
    wts = jax.nn.softmax(jnp.stack(lses), axis=0)
    return jnp.sum(wts[..., None] * jnp.stack(outs), axis=0)


def attn_conv_mixer(x, w_in, conv_w, w_out, pos):
    B, S, _ = x.shape
    h = x @ w_in
    idx = np.cumsum([A_WIDTH, A_WIDTH, A_WIDTH, B_WIDTH, B_WIDTH])
    q, k, v, gb, gc, hin = jnp.split(h, idx, axis=-1)
    q = rope(q.reshape(B, S, A_HEADS, HEAD_DIM), pos)
    k = rope(k.reshape(B, S, A_HEADS, HEAD_DIM), pos)
    v = v.reshape(B, S, A_HEADS, HEAD_DIM)
    a = dilated_mixture(q, k, v).reshape(B, S, A_WIDTH).astype(x.dtype)
    u = gc * hin
    y = lax.conv_general_dilated(u, conv_w[:, None, :], window_strides=(1,),
                                 padding=((CONV_WIDTH - 1, 0),),
                                 dimension_numbers=('NWC', 'WIO', 'NWC'),
                                 feature_group_count=B_WIDTH)
    b = gb * y
    return jnp.concatenate([a, b], axis=-1) @ w_out


def mla_mixer(x, w_in, q_norm, kv_norm, w_uq, w_ukv, w_out, pos):
    B, S, _ = x.shape
    h = x @ w_in
    cq, ckv, kr = jnp.split(h, [Q_LORA, Q_LORA + KV_LORA], axis=-1)
    q = (rmsnorm(cq, q_norm) @ w_uq).reshape(B, S, MLA_HEADS, QK_NOPE + QK_ROPE)
    qn = q[..., :QK_NOPE]
    qr = rope(q[..., QK_NOPE:], pos)
    kv = (rmsnorm(ckv, kv_norm) @ w_ukv).reshape(B, S, MLA_HEADS, QK_NOPE + V_DIM)
    kn = kv[..., :QK_NOPE].astype(jnp.float32)
    vh = kv[..., QK_NOPE:].astype(jnp.float32)
    kr = rope(kr[:, :, None, :], pos)[:, :, 0].astype(jnp.float32)
    scale = (QK_NOPE + QK_ROPE) ** -0.5
    nq = S // Q_BLOCK
    kpos = jnp.arange(S)

    def blockify(t):
        return t.reshape(B, nq, Q_BLOCK, *t.shape[2:]).swapaxes(0, 1)

    def one_block(args):
        qn_b, qr_b, i = args
        s = (jnp.einsum('bqhd,bkhd->bhqk', qn_b.astype(jnp.float32), kn)
             + jnp.einsum('bqhr,bkr->bhqk', qr_b.astype(jnp.float32), kr)) * scale
        qpos = i * Q_BLOCK + jnp.arange(Q_BLOCK)
        s = jnp.where(kpos[None, :] <= qpos[:, None], s, NEG)
        p = jax.nn.softmax(s, axis=-1)
        return jnp.einsum('bhqk,bkhd->bqhd', p, vh)

    o = lax.map(one_block, (blockify(qn), blockify(qr), jnp.arange(nq)))
    o = o.swapaxes(0, 1).reshape(B, S, MLA_HEADS * V_DIM).astype(x.dtype)
    return o @ w_out


def swiglu(x, w_gate, w_up, w_down):
    return (jax.nn.silu(x @ w_gate) * (x @ w_up)) @ w_down


def setup_inputs(seed: int = 0) -> dict:
    key = jax.random.key(seed)
    ks = jax.random.split(key, 20)
    f32 = jnp.float32

    def nrm(k, shape, fan_in, mult=1.0):
        return jax.random.normal(k, shape, f32) * (fan_in ** -0.5) * mult

    d = D_MODEL
    return {
        "x": jax.random.normal(ks[0], (BATCH, SEQ, d), f32),
        "w_in_a": nrm(ks[1], (N_EVEN, d, 3 * A_WIDTH + 3 * B_WIDTH), d),
        "conv_w": nrm(ks[2], (N_EVEN, CONV_WIDTH, B_WIDTH), CONV_WIDTH),
        "w_out_a": nrm(ks[3], (N_EVEN, A_WIDTH + B_WIDTH, d), A_WIDTH + B_WIDTH, BETA),
        "w_in_c": nrm(ks[4], (N_ODD, d, MLA_IN), d),
        "q_norm": 1.0 + 0.02 * jax.random.normal(ks[5], (N_ODD, Q_LORA), f32),
        "kv_norm": 1.0 + 0.02 * jax.random.normal(ks[6], (N_ODD, KV_LORA), f32),
        "w_uq": nrm(ks[7], (N_ODD, Q_LORA, MLA_HEADS * (QK_NOPE + QK_ROPE)), Q_LORA),
        "w_ukv": nrm(ks[8], (N_ODD, KV_LORA, MLA_HEADS * (QK_NOPE + V_DIM)), KV_LORA),
        "w_out_c": nrm(ks[9], (N_ODD, MLA_HEADS * V_DIM, d), MLA_HEADS * V_DIM, BETA),
        "ln1_g": 1.0 + 0.02 * jax.random.normal(ks[10], (DEPTH, d), f32),
        "ln1_b": 0.02 * jax.random.normal(ks[11], (DEPTH, d), f32),
        "w_gate": nrm(ks[12], (DEPTH, d, FFN_HIDDEN), d),
        "w_up": nrm(ks[13], (DEPTH, d, FFN_HIDDEN), d),
        "w_down": nrm(ks[14], (DEPTH, FFN_HIDDEN, d), FFN_HIDDEN, BETA),
        "ln2_g": 1.0 + 0.02 * jax.random.normal(ks[15], (DEPTH, d), f32),
        "ln2_b": 0.02 * jax.random.normal(ks[16], (DEPTH, d), f32),
    }


def reference(x, w_in_a, conv_w, w_out_a, w_in_c, q_norm, kv_norm, w_uq, w_ukv,
              w_out_c, ln1_g, ln1_b, w_gate, w_up, w_down, ln2_g, ln2_b):
    pos = jnp.arange(x.shape[1])
    for l in range(DEPTH):
        j = l // 2
        if l % 2 == 0:
            mix = attn_conv_mixer(x, w_in_a[j], conv_w[j], w_out_a[j], pos)
        else:
            mix = mla_mixer(x, w_in_c[j], q_norm[j], kv_norm[j], w_uq[j], w_ukv[j],
                            w_out_c[j], pos)
        x = layernorm(ALPHA * x + mix, ln1_g[l], ln1_b[l])
        x = layernorm(ALPHA * x + swiglu(x, w_gate[l], w_up[l], w_down[l]), ln2_g[l], ln2_b[l])
    return x
```

```python
import math
from contextlib import ExitStack

import numpy as np
import ml_dtypes

import concourse.bass as bass
import concourse.mybir as mybir
from concourse.bass_utils import run_bass_kernel_spmd

F32 = mybir.dt.float32
BF16 = mybir.dt.bfloat16
AF = mybir.ActivationFunctionType
ALU = mybir.AluOpType
NPBF = ml_dtypes.bfloat16

NCORES = 8
D = 4096
S = 16384
NTC = 2
TOK = S // NTC
T = 512
KC = D // 128
A_WIDTH = 2048
FFN = 11008
FC = FFN // 128
ALPHA = 4.0 ** 0.25
LN_EPS = 1e-5
RMS_EPS = 1e-6
THETA = 10000.0
Q_LORA, KV_LORA, QK_ROPE = 1536, 512, 64
MLA_IN = Q_LORA + KV_LORA + QK_ROPE
MLA_H = 32
PI = math.pi


class Buf:
    __slots__ = ("w", "r")

    def __init__(self):
        self.w = None
        self.r = {}


class KB:
    NDS = 48

    def __init__(self, nc, st):
        self.nc = nc
        self.st = st
        self.sems = []
        self.E = {}
        for name, eng in (("pe", nc.tensor), ("act", nc.scalar), ("dve", nc.vector),
                          ("pool", nc.gpsimd), ("sp", nc.sync)):
            sem = st.enter_context(nc.semaphore("s_" + name))
            self.sems.append(sem)
            self.E[name] = dict(eng=eng, sid=len(self.sems) - 1, cnt=0, waited={})
        self.dsid = []
        for i in range(self.NDS):
            sem = st.enter_context(nc.semaphore("d%d" % i))
            self.sems.append(sem)
            self.dsid.append(len(self.sems) - 1)
        self.dval = [0] * self.NDS
        self.dnext = 0
        self.nbuf = 0

    def sb(self, name, shape, dt):
        return self.st.enter_context(self.nc.sbuf_tensor(name, shape, dt))

    def ps(self, name, shape=(128, 512), dt=F32):
        return self.st.enter_context(self.nc.psum_tensor(name, list(shape), dt))

    def wait(self, en, ev, is_dma=False):
        if ev is None:
            return
        sid, v = ev
        e = self.E[en]
        if sid == e["sid"] and en == "pe":
            return
        if e["waited"].get(sid, 0) >= v:
            return
        e["eng"].wait_ge(self.sems[sid], v)
        e["waited"][sid] = v

    def _deps(self, en, reads, writes, is_dma=False):
        for b in reads:
            self.wait(en, b.w, is_dma)
        for b in writes:
            self.wait(en, b.w, is_dma)
            for sid, v in b.r.items():
                self.wait(en, (sid, v), is_dma)

    def _mark(self, ev, reads, writes):
        sid, v = ev
        for b in reads:
            if b.r.get(sid, 0) < v:
                b.r[sid] = v
        for b in writes:
            b.w = ev
            b.r = {}

    def op(self, en, fn, reads=(), writes=()):
        self._deps(en, reads, writes)
        e = self.E[en]
        ins = fn(e["eng"])
        e["cnt"] += 1
        ins.then_inc(self.sems[e["sid"]], 1)
        self._mark((e["sid"], e["cnt"]), reads, writes)

    def mm(self, fns, reads=(), writes=()):
        self._deps("pe", reads, writes)
        e = self.E["pe"]
        ins = None
        for fn in fns:
            ins = fn(e["eng"])
        e["cnt"] += 1
        ins.then_inc(self.sems[e["sid"]], 1)
        self._mark((e["sid"], e["cnt"]), reads, writes)

    def dma(self, qn, out, in_, reads=(), writes=(), track=None):
        self._deps(qn, reads, writes, is_dma=True)
        i = self.dnext
        self.dnext = (i + 1) % self.NDS
        if self.dval[i] > 0:
            self.wait(qn, (self.dsid[i], self.dval[i]), True)
        self.dval[i] += 16
        self.E[qn]["eng"].dma_start(out=out, in_=in_).then_inc(self.sems[self.dsid[i]], 16)
        self._mark((self.dsid[i], self.dval[i]), reads, writes)
        if track is not None:
            track[self.dsid[i]] = self.dval[i]

    def finish(self, track):
        for sid, v in track.items():
            self.wait("sp", (sid, v), True)


class Ring:
    def __init__(self, kb, name, n, shape, dt, psum=False):
        self.t = []
        for i in range(n):
            t = kb.ps(name + str(i), shape, dt) if psum else kb.sb(name + str(i), list(shape), dt)
            self.t.append((t, Buf()))
        self.i = 0

    def next(self):
        r = self.t[self.i]
        self.i = (self.i + 1) % len(self.t)
        return r


def w_src(w, c0, ncols, kc=None):
    return w[:, c0:c0 + ncols].rearrange("(kc p) f -> p kc f", p=128)


def make_perm(kb, perm, n):
    h = n // 2
    b = Buf()
    kb.op("pool", lambda g: g.memset(perm[:], 0.0), writes=[b])
    kb.op("pool", lambda g: g.affine_select(out=perm[0:n, 0:h], in_=perm[0:n, 0:h], pattern=[[-1, h]],
                                             compare_op=ALU.not_equal, fill=1.0, base=-h,
                                             channel_multiplier=1), writes=[b])
    kb.op("pool", lambda g: g.affine_select(out=perm[0:n, h:n], in_=perm[0:n, h:n], pattern=[[-1, h]],
                                             compare_op=ALU.not_equal, fill=1.0, base=0,
                                             channel_multiplier=1), writes=[b])
    return b


class Rope:
    def __init__(self, kb, n, tok0_ap):
        self.kb, self.n = kb, n
        h = n // 2
        self.h = h
        I32 = mybir.dt.int32
        self.jidx = kb.sb("rp_j%d" % n, [128, T], F32)
        self.pidx = kb.sb("rp_p%d" % n, [128, 1], F32)
        self.pm = kb.sb("rp_pm%d" % n, [128, 1], F32)
        self.invf = kb.sb("rp_f%d" % n, [128, 1], F32)
        self.tokb = kb.sb("rp_tb%d" % n, [128, 1], F32)
        self.tok0 = kb.sb("rp_t0%d" % n, [128, 1], F32)
        self.r = kb.sb("rp_r%d" % n, [128, T], F32)
        self.ni = kb.sb("rp_ni%d" % n, [128, T], I32)
        self.nf = kb.sb("rp_nf%d" % n, [128, T], F32)
        self.f = kb.sb("rp_fr%d" % n, [128, T], F32)
        self.C = kb.sb("rp_c%d" % n, [128, T], F32)
        self.Sg = kb.sb("rp_s%d" % n, [128, T], F32)
        self.bconst = Buf()
        self.btab = Buf()
        self.btmp = Buf()
        kb.dma("sp", self.tok0[:], tok0_ap, writes=[self.bconst])
        kb.op("pool", lambda g: g.iota(self.jidx[:], [[1, T]], base=0, channel_multiplier=0,
                                       allow_small_or_imprecise_dtypes=True), writes=[self.bconst])
        kb.op("pool", lambda g: g.iota(self.pidx[:], [[0, 1]], base=0, channel_multiplier=1,
                                       allow_small_or_imprecise_dtypes=True), writes=[self.bconst])
        kb.op("dve", lambda v: v.tensor_single_scalar(out=self.pm[:], in_=self.pidx[:], scalar=float(h),
                                                      op=ALU.is_ge), reads=[self.bconst], writes=[self.btmp])
        kb.op("dve", lambda v: v.scalar_tensor_tensor(out=self.pidx[:], in0=self.pm[:], scalar=-float(h),
                                                      in1=self.pidx[:], op0=ALU.mult, op1=ALU.add),
              writes=[self.bconst])
        kb.op("act", lambda a: a.activation(out=self.invf[:], in_=self.pidx[:], func=AF.Exp,
                                            scale=-math.log(THETA) / h), reads=[self.bconst], writes=[self.btmp])
        kb.op("dve", lambda v: v.tensor_scalar_mul(out=self.invf[:], in0=self.invf[:], scalar1=1.0 / (2 * PI)),
              reads=[self.btmp], writes=[self.bconst])

    def _frac(self):
        kb = self.kb
        kb.op("dve", lambda v: v.tensor_copy(out=self.ni[:], in_=self.r[:]), writes=[self.btmp])
        kb.op("dve", lambda v: v.tensor_copy(out=self.nf[:], in_=self.ni[:]), writes=[self.btmp])
        kb.op("dve", lambda v: v.tensor_tensor(out=self.f[:], in0=self.r[:], in1=self.nf[:], op=ALU.subtract),
              writes=[self.btmp])

    def tile(self, t0):
        kb, n, h = self.kb, self.n, self.h
        kb.op("dve", lambda v: v.tensor_scalar_add(out=self.tokb[:], in0=self.tok0[:], scalar1=float(t0)),
              reads=[self.bconst], writes=[self.btmp])
        kb.op("dve", lambda v: v.tensor_scalar(out=self.r[:], in0=self.jidx[:], scalar1=self.tokb[:, 0:1],
                                               scalar2=self.invf[:, 0:1], op0=ALU.add, op1=ALU.mult),
              reads=[self.bconst], writes=[self.btmp])
        self._frac()
        kb.op("act", lambda a: a.activation(out=self.Sg[0:h, :], in_=self.f[0:h, :], func=AF.Sin, scale=-2 * PI),
              reads=[self.btmp], writes=[self.btab])
        kb.op("act", lambda a: a.activation(out=self.Sg[h:n, :], in_=self.f[h:n, :], func=AF.Sin, scale=2 * PI),
              reads=[self.btmp], writes=[self.btab])
        kb.op("dve", lambda v: v.tensor_scalar_add(out=self.r[:], in0=self.r[:], scalar1=0.25), writes=[self.btmp])
        self._frac()
        kb.op("act", lambda a: a.activation(out=self.C[0:n, :], in_=self.f[0:n, :], func=AF.Sin, scale=2 * PI),
              reads=[self.btmp], writes=[self.btab])


def build_l1(ntiles=TOK // T, blocks_sel=None):
    nc = bass.Bass("TRN2", target_bir_lowering=False)
    xT = nc.dram_tensor("xT", [D, TOK], F32, kind="ExternalInput").ap()
    w = nc.dram_tensor("w_in", [D, 3 * A_WIDTH + 3 * A_WIDTH], F32, kind="ExternalInput").ap()
    tok0 = nc.dram_tensor("tok0", [128, 1], F32, kind="ExternalInput").ap()
    qT = nc.dram_tensor("qT", [A_WIDTH, TOK], BF16, kind="ExternalOutput").ap()
    kT = nc.dram_tensor("kT", [A_WIDTH, TOK], BF16, kind="ExternalOutput").ap()
    vT = nc.dram_tensor("vT", [A_WIDTH, TOK], BF16, kind="ExternalOutput").ap()
    gbT = nc.dram_tensor("gbT", [A_WIDTH, TOK], F32, kind="ExternalOutput").ap()
    uT = nc.dram_tensor("uT", [A_WIDTH, TOK], F32, kind="ExternalOutput").ap()
    with ExitStack() as st:
        kb = KB(nc, st)
        perm = kb.sb("perm", [128, 128], BF16)
        bperm = make_perm(kb, perm, 128)
        rope = Rope(kb, 128, tok0)
        xring = Ring(kb, "xb", 2, (128, KC, T), BF16)
        wring = Ring(kb, "wb", 2, (128, KC, 512), BF16)
        psr = Ring(kb, "ps", 6, (128, 512), F32, psum=True)
        pswr = Ring(kb, "psw", 2, (128, 512), F32, psum=True)
        qbr = Ring(kb, "qb", 2, (128, T), BF16)
        t1r = Ring(kb, "t1", 2, (128, T), F32)
        t2r = Ring(kb, "t2", 2, (128, T), F32)
        obr = Ring(kb, "ob", 4, (128, T), BF16)
        ofr = Ring(kb, "of", 4, (128, T), F32)
        gcb = [(kb.sb("gc%d" % i, [128, T], F32), Buf()) for i in range(4)]
        bout = {}
        blocks = []
        for typ, base in (("q", 0), ("k", 2048), ("v", 4096), ("gb", 6144)):
            for i in range(4):
                blocks.append((typ, base + 512 * i, i))
        for i in range(4):
            blocks.append(("gc", 8192 + 512 * i, i))
            blocks.append(("hin", 10240 + 512 * i, i))
        outs = {"q": qT, "k": kT, "v": vT, "gb": gbT, "hin": uT}

        def load_x(ti):
            xt, xbuf = xring.next()
            kb.dma("pool", xt[:], xT[:, ti * T:(ti + 1) * T].rearrange("(kc p) t -> p kc t", p=128),
                   writes=[xbuf])
            return xt, xbuf

        nxt = load_x(0)
        if blocks_sel is not None:
            blocks = [blocks[i] for i in blocks_sel]
        for ti in range(ntiles):
            xt, xbuf = nxt
            rope.tile(ti * T)
            for bi, (typ, c0, i) in enumerate(blocks):
                wt, wbuf = wring.next()
                kb.dma("pool", wt[:], w_src(w, c0, 512), writes=[wbuf])
                if bi == min(4, len(blocks) - 1) and ti + 1 < ntiles:
                    nxt = load_x(ti + 1)
                for half in range(2):
                    pss = [psr.next() for _ in range(2)]
                    fns = []
                    for k in range(KC):
                        for j in range(2):
                            cc = (half * 2 + j) * 128
                            fns.append(lambda pe, k=k, j=j, cc=cc: pe.matmul(
                                pss[j][0][:], wt[:, k, cc:cc + 128], xt[:, k, :],
                                start=(k == 0), stop=(k == KC - 1)))
                    kb.mm(fns, reads=[wbuf, xbuf], writes=[pss[0][1], pss[1][1]])
                    for j in range(2):
                        pt, pbuf = pss[j]
                        ch = half * 2 + j
                        row0 = (c0 % 2048) + ch * 128
                        cols = slice(ti * T, (ti + 1) * T)
                        if typ in ("q", "k"):
                            qb, qbuf = qbr.next()
                            kb.op("act", lambda a: a.copy(out=qb[:], in_=pt[:]), reads=[pbuf], writes=[qbuf])
                            pw, pwbuf = pswr.next()
                            kb.mm([lambda pe: pe.matmul(pw[:], perm[:], qb[:], start=True, stop=True)],
                                  reads=[bperm, qbuf], writes=[pwbuf])
                            t1, t1b = t1r.next()
                            t2, t2b = t2r.next()
                            ob, obb = obr.next()
                            kb.op("dve", lambda v: v.tensor_tensor(out=t1[:], in0=qb[:], in1=rope.C[:], op=ALU.mult),
                                  reads=[qbuf, rope.btab], writes=[t1b])
                            kb.op("dve", lambda v: v.tensor_tensor(out=t2[:], in0=pw[:], in1=rope.Sg[:], op=ALU.mult),
                                  reads=[pwbuf, rope.btab], writes=[t2b])
                            kb.op("pool", lambda g: g.tensor_tensor(out=ob[:], in0=t1[:], in1=t2[:], op=ALU.add),
                                  reads=[t1b, t2b], writes=[obb])
                            kb.dma("sp", outs[typ][row0:row0 + 128, cols], ob[:], reads=[obb], track=bout)
                        elif typ == "v":
                            ob, obb = obr.next()
                            kb.op("act", lambda a: a.copy(out=ob[:], in_=pt[:]), reads=[pbuf], writes=[obb])
                            kb.dma("sp", vT[row0:row0 + 128, cols], ob[:], reads=[obb], track=bout)
                        elif typ == "gb":
                            of, ofb = ofr.next()
                            kb.op("act", lambda a: a.copy(out=of[:], in_=pt[:]), reads=[pbuf], writes=[ofb])
                            kb.dma("sp", gbT[row0:row0 + 128, cols], of[:], reads=[ofb], track=bout)
                        elif typ == "gc":
                            gt, gbuf = gcb[ch]
                            kb.op("act", lambda a: a.copy(out=gt[:], in_=pt[:]), reads=[pbuf], writes=[gbuf])
                        else:
                            gt, gbuf = gcb[ch]
                            of, ofb = ofr.next()
                            kb.op("dve", lambda v: v.tensor_tensor(out=of[:], in0=gt[:], in1=pt[:], op=ALU.mult),
                                  reads=[pbuf, gbuf], writes=[ofb])
                            kb.dma("sp", uT[row0:row0 + 128, cols], of[:], reads=[ofb], track=bout)
        kb.finish(bout)
    return nc


def run_l1(xTfull, w_in_a):
    nc = build_l1()
    in_maps = []
    for c in range(NTC):
        in_maps.append({"xT": np.ascontiguousarray(xTfull[:, c * TOK:(c + 1) * TOK]),
                        "w_in": w_in_a, "tok0": np.full((128, 1), c * TOK, np.float32)})
    res = run_bass_kernel_spmd(nc, in_maps, core_ids=list(range(NTC)))
    out = {}
    for name in ("qT", "kT", "vT", "gbT", "uT"):
        out[name] = np.concatenate([r[name] for r in res.results], axis=1)
    return out


def barrier(kb):
    evs = [(e["sid"], e["cnt"]) for e in kb.E.values() if e["cnt"] > 0]
    evs += [(kb.dsid[i], kb.dval[i]) for i in range(kb.NDS) if kb.dval[i] > 0]
    for en in kb.E:
        for ev in evs:
            kb.wait(en, ev, is_dma=True)


def linear(kb, xt, xbuf, kcn, w, groups, wring, psr, evac, wq="pool"):
    for grp in groups:
        g0 = grp[0][0]
        gw = grp[-1][0] + grp[-1][1] - g0
        wt, wbuf = wring.next()
        kb.dma(wq, wt[:, 0:kcn, 0:gw], w[:, g0:g0 + gw].rearrange("(kc p) f -> p kc f", p=128), writes=[wbuf])
        pss = [psr.next() for _ in grp]
        fns = []
        for k in range(kcn):
            for j, (c0, m) in enumerate(grp):
                fns.append(lambda pe, k=k, j=j, c0=c0, m=m: pe.matmul(
                    pss[j][0][0:m, :], wt[:, k, c0 - g0:c0 - g0 + m], xt[:, k, :],
                    start=(k == 0), stop=(k == kcn - 1)))
        kb.mm(fns, reads=[wbuf, xbuf], writes=[p[1] for p in pss])
        for j, (c0, m) in enumerate(grp):
            evac(c0, m, pss[j][0], pss[j][1])


class Norm:
    def __init__(self, kb, psr, tag):
        self.kb, self.psr = kb, psr
        self.ones = kb.sb("n1_" + tag, [128, 128], F32)
        self.bones = Buf()
        kb.op("pool", lambda g: g.memset(self.ones[:], 1.0), writes=[self.bones])
        self.sq = Ring(kb, "nsq_" + tag, 2, (128, T), F32)
        self.mean = kb.sb("nmu_" + tag, [128, T], F32)
        self.rstd = kb.sb("nrs_" + tag, [128, T], F32)
        self.tmp = Ring(kb, "ntp_" + tag, 2, (128, T), F32)
        self.bstat = Buf()

    def stats(self, yT, ybufs, nch, nfeat, eps, center):
        kb = self.kb
        p2, p2b = self.psr.next()
        if center:
            p1, p1b = self.psr.next()
            kb.mm([lambda pe, c=c: pe.matmul(p1[:], self.ones[:], yT[:, c, :], start=(c == 0), stop=(c == nch - 1))
                   for c in range(nch)], reads=[self.bones] + ybufs, writes=[p1b])
        for c in range(nch):
            sq, sqb = self.sq.next()
            kb.op("act", lambda a: a.activation(out=sq[:], in_=yT[:, c, :], func=AF.Square),
                  reads=[ybufs[c]], writes=[sqb])
            kb.mm([lambda pe: pe.matmul(p2[:], self.ones[:], sq[:], start=(c == 0), stop=(c == nch - 1))],
                  reads=[self.bones, sqb], writes=[p2b])
        inv = 1.0 / nfeat
        if center:
            kb.op("dve", lambda v: v.tensor_scalar_mul(out=self.mean[:], in0=p1[:], scalar1=inv),
                  reads=[p1b], writes=[self.bstat])
            tm, tmb = self.tmp.next()
            kb.op("dve", lambda v: v.tensor_tensor(out=tm[:], in0=self.mean[:], in1=self.mean[:], op=ALU.mult),
                  reads=[self.bstat], writes=[tmb])
            kb.op("dve", lambda v: v.scalar_tensor_tensor(out=self.rstd[:], in0=p2[:], scalar=inv, in1=tm[:],
                                                          op0=ALU.mult, op1=ALU.subtract),
                  reads=[p2b, tmb], writes=[self.bstat])
            kb.op("dve", lambda v: v.tensor_scalar_add(out=self.rstd[:], in0=self.rstd[:], scalar1=eps),
                  writes=[self.bstat])
        else:
            kb.op("dve", lambda v: v.tensor_scalar(out=self.rstd[:], in0=p2[:], scalar1=inv, scalar2=eps,
                                                   op0=ALU.mult, op1=ALU.add), reads=[p2b], writes=[self.bstat])
        kb.op("act", lambda a: a.activation(out=self.rstd[:], in_=self.rstd[:], func=AF.Sqrt),
              reads=[self.bstat], writes=[self.bstat])
        kb.op("dve", lambda v: v.reciprocal(out=self.rstd[:], in_=self.rstd[:]), reads=[self.bstat],
              writes=[self.bstat])

    def apply(self, src, srcbuf, c, gam, bet, center, outs):
        kb = self.kb
        tm, tmb = self.tmp.next()
        if center:
            kb.op("dve", lambda v: v.tensor_tensor(out=tm[:], in0=src, in1=self.mean[:], op=ALU.subtract),
                  reads=[srcbuf, self.bstat], writes=[tmb])
            kb.op("dve", lambda v: v.tensor_tensor(out=tm[:], in0=tm[:], in1=self.rstd[:], op=ALU.mult),
                  reads=[self.bstat], writes=[tmb])
        else:
            kb.op("dve", lambda v: v.tensor_tensor(out=tm[:], in0=src, in1=self.rstd[:], op=ALU.mult),
                  reads=[srcbuf, self.bstat], writes=[tmb])
        for ap, buf in outs:
            if bet is not None:
                kb.op("act", lambda a: a.activation(out=ap, in_=tm[:], func=AF.Identity, scale=gam[:, c:c + 1],
                                                    bias=bet[:, c:c + 1]), reads=[tmb], writes=[buf])
            else:
                kb.op("act", lambda a: a.activation(out=ap, in_=tm[:], func=AF.Identity, scale=gam[:, c:c + 1]),
                      reads=[tmb], writes=[buf])


def load_cols(kb, name, ap, n):
    t = kb.sb(name, [128, n], F32)
    b = Buf()
    kb.dma("sp", t[:], ap, writes=[b])
    return t, b


def build_chain(first, ntiles=TOK // T, do_ffn=True, do_mla=True):
    nc = bass.Bass("TRN2", target_bir_lowering=False)
    dt_in = nc.dram_tensor
    inT = dt_in("inT", [D if not first else A_WIDTH, TOK], BF16, kind="ExternalInput").ap()
    xT = dt_in("xT", [D, TOK], F32, kind="ExternalInput").ap()
    w_out = dt_in("w_out", [D, D], F32, kind="ExternalInput").ap()
    w_gate = dt_in("w_gate", [D, FFN], F32, kind="ExternalInput").ap()
    w_up = dt_in("w_up", [D, FFN], F32, kind="ExternalInput").ap()
    w_down = dt_in("w_down", [FFN, D], F32, kind="ExternalInput").ap()
    lnp = dt_in("lnp", [128, 4 * KC], F32, kind="ExternalInput").ap()
    if first:
        uTh = dt_in("uTh", [A_WIDTH, TOK + 2], F32, kind="ExternalInput").ap()
        gbT = dt_in("gbT", [A_WIDTH, TOK], F32, kind="ExternalInput").ap()
        cwp = dt_in("cwp", [128, 16 * 3], F32, kind="ExternalInput").ap()
        tok0 = dt_in("tok0", [128, 1], F32, kind="ExternalInput").ap()
        w_inc = dt_in("w_inc", [D, MLA_IN], F32, kind="ExternalInput").ap()
        w_uq = dt_in("w_uq", [Q_LORA, MLA_H * 192], F32, kind="ExternalInput").ap()
        w_ukv = dt_in("w_ukv", [KV_LORA, MLA_H * 256], F32, kind="ExternalInput").ap()
        nrm = dt_in("nrm", [128, 16], F32, kind="ExternalInput").ap()
        QnT = dt_in("QnT", [MLA_H * 128, TOK], BF16, kind="ExternalOutput").ap()
        QrT = dt_in("QrT", [MLA_H * 64, TOK], BF16, kind="ExternalOutput").ap()
        KnT = dt_in("KnT", [MLA_H * 128, TOK], BF16, kind="ExternalOutput").ap()
        VT = dt_in("VT", [MLA_H * 128, TOK], BF16, kind="ExternalOutput").ap()
        KrT = dt_in("KrT", [64, TOK], BF16, kind="ExternalOutput").ap()
    x2T = dt_in("x2T", [D, TOK], F32, kind="ExternalOutput").ap()
    X1 = dt_in("X1", [D, TOK], F32, kind="Internal").ap()
    X1B = dt_in("X1B", [D, TOK], BF16, kind="Internal").ap()
    H = dt_in("H", [FFN, TOK], BF16, kind="Internal").ap()
    track = {}
    bX1 = [Buf() for _ in range(ntiles)]
    bX1B = [Buf() for _ in range(ntiles)]
    bH = [[Buf() for _ in range(FC)] for _ in range(ntiles)]
    bX2 = [Buf() for _ in range(ntiles)]

    def cs(ti):
        return slice(ti * T, (ti + 1) * T)

    def cm(ap):
        return ap.rearrange("(c p) t -> p c t", p=128)

    with ExitStack() as st0:
        kb = KB(nc, st0)
        psr = Ring(kb, "ps", 8, (128, 512), F32, psum=True)
        lnt, _ = load_cols(kb, "lnp_sb", lnp, 4 * KC)
        if first:
            cwt, _ = load_cols(kb, "cwp_sb", cwp, 48)
            nrt, _ = load_cols(kb, "nrm_sb", nrm, 16)
        barrier(kb)

        with ExitStack() as st:
            kb.st = st
            norm = Norm(kb, psr, "a")
            inb = Ring(kb, "inb", 1, (128, KC, T), BF16)
            wring = Ring(kb, "w1", 2, (128, KC, 256), BF16)
            yT = kb.sb("yT", [128, KC, T], F32)
            ybufs = [Buf() for _ in range(KC)]
            xres = Ring(kb, "xres", 2, (128, 2, T), F32)
            if first:
                ur = Ring(kb, "ur", 2, (128, T + 2), F32)
                gr = Ring(kb, "gr", 2, (128, T), F32)
                ct = Ring(kb, "ct", 2, (128, T), F32)
            for ti in range(ntiles):
                it, ibuf = inb.next()
                if first:
                    kb.dma("sp", it[:, 0:16, :], cm(inT[:, cs(ti)]), writes=[ibuf])
                    for c in range(16):
                        ut, ub = ur.next()
                        gt, gb_ = gr.next()
                        tt, tb = ct.next()
                        kb.dma("sp", ut[:], uTh[c * 128:(c + 1) * 128, ti * T:ti * T + T + 2], writes=[ub])
                        kb.dma("sp", gt[:], gbT[c * 128:(c + 1) * 128, cs(ti)], writes=[gb_])
                        kb.op("dve", lambda v: v.tensor_scalar_mul(out=tt[:], in0=ut[:, 2:T + 2],
                                                                   scalar1=cwt[:, 3 * c + 2:3 * c + 3]),
                              reads=[ub], writes=[tb])
                        kb.op("dve", lambda v: v.scalar_tensor_tensor(out=tt[:], in0=ut[:, 1:T + 1],
                                                                      scalar=cwt[:, 3 * c + 1:3 * c + 2], in1=tt[:],
                                                                      op0=ALU.mult, op1=ALU.add),
                              reads=[ub], writes=[tb])
                        kb.op("dve", lambda v: v.scalar_tensor_tensor(out=tt[:], in0=ut[:, 0:T],
                                                                      scalar=cwt[:, 3 * c:3 * c + 1], in1=tt[:],
                                                                      op0=ALU.mult, op1=ALU.add),
                              reads=[ub], writes=[tb])
                        kb.op("dve", lambda v: v.tensor_tensor(out=it[:, 16 + c, :], in0=tt[:], in1=gt[:], op=ALU.mult),
                              reads=[tb, gb_], writes=[ibuf])
                else:
                    kb.dma("sp", it[:], cm(inT[:, cs(ti)]), writes=[ibuf])

                def evac1(c0, m, pt, pbuf):
                    ch = c0 // 128
                    if ch % 2 == 0:
                        evac1.x = xres.next()
                        kb.dma("sp", evac1.x[0][:], cm(xT[ch * 128:(ch + 2) * 128, cs(ti)]), writes=[evac1.x[1]])
                    xt_, xb_ = evac1.x
                    kb.op("dve", lambda v: v.scalar_tensor_tensor(out=yT[:, ch, :], in0=xt_[:, ch % 2, :], scalar=ALPHA,
                                                                  in1=pt[:], op0=ALU.mult, op1=ALU.add),
                          reads=[pbuf, xb_], writes=[ybufs[ch]])
                groups = [[(g * 256, 128), (g * 256 + 128, 128)] for g in range(16)]
                linear(kb, it, ibuf, KC, w_out, groups, wring, psr, evac1)
                norm.stats(yT, ybufs, KC, D, LN_EPS, True)
                for c in range(KC):
                    norm.apply(yT[:, c, :], ybufs[c], c, lnt[:, 0:KC], lnt[:, KC:2 * KC], True,
                               [(yT[:, c, :], ybufs[c])])
                    kb.op("act", lambda a: a.copy(out=it[:, c, :], in_=yT[:, c, :]), reads=[ybufs[c]], writes=[ibuf])
                kb.dma("sp", cm(X1[:, cs(ti)]), yT[:], reads=ybufs, writes=[bX1[ti]])
                kb.dma("sp", cm(X1B[:, cs(ti)]), it[:], reads=[ibuf], writes=[bX1B[ti]])
            barrier(kb)

        if do_ffn:
            with ExitStack() as st:
                kb.st = st
                xbr = Ring(kb, "x1b", 2, (128, KC, T), BF16)
                wg = Ring(kb, "wg", 2, (128, KC, 256), BF16)
                wu = Ring(kb, "wu", 2, (128, KC, 256), BF16)
                sgr = Ring(kb, "sg", 2, (128, T), F32)
                hbr = Ring(kb, "hb", 4, (128, T), BF16)
                for ti in range(ntiles):
                    xt, xbuf = xbr.next()
                    kb.dma("sp", xt[:], cm(X1B[:, cs(ti)]), reads=[bX1B[ti]], writes=[xbuf])
                    for g in range(FC // 2):
                        wgt, wgb = wg.next()
                        wut, wub = wu.next()
                        kb.dma("pool", wgt[:], w_src(w_gate, g * 256, 256), writes=[wgb])
                        kb.dma("pool", wut[:], w_src(w_up, g * 256, 256), writes=[wub])
                        pg = [psr.next() for _ in range(2)]
                        pu = [psr.next() for _ in range(2)]
                        fns = []
                        for k in range(KC):
                            for j in range(2):
                                fns.append(lambda pe, k=k, j=j: pe.matmul(pg[j][0][:], wgt[:, k, j * 128:(j + 1) * 128],
                                                                          xt[:, k, :], start=(k == 0), stop=(k == KC - 1)))
                                fns.append(lambda pe, k=k, j=j: pe.matmul(pu[j][0][:], wut[:, k, j * 128:(j + 1) * 128],
                                                                          xt[:, k, :], start=(k == 0), stop=(k == KC - 1)))
                        kb.mm(fns, reads=[wgb, wub, xbuf], writes=[p[1] for p in pg + pu])
                        for j in range(2):
                            fch = g * 2 + j
                            sg, sgb = sgr.next()
                            hb, hbb = hbr.next()
                            kb.op("act", lambda a: a.activation(out=sg[:], in_=pg[j][0][:], func=AF.Silu),
                                  reads=[pg[j][1]], writes=[sgb])
                            kb.op("dve", lambda v: v.tensor_tensor(out=hb[:], in0=sg[:], in1=pu[j][0][:], op=ALU.mult),
                                  reads=[sgb, pu[j][1]], writes=[hbb])
                            kb.dma("sp", H[fch * 128:(fch + 1) * 128, cs(ti)], hb[:], reads=[hbb], writes=[bH[ti][fch]])
                barrier(kb)

            with ExitStack() as st:
                kb.st = st
                norm = Norm(kb, psr, "b")
                hT = kb.sb("hT", [128, FC, T], BF16)
                hbuf = Buf()
                wd = Ring(kb, "wd", 2, (128, 8, 512), BF16)
                yT = kb.sb("yT2", [128, KC, T], F32)
                ybufs = [Buf() for _ in range(KC)]
                xres = Ring(kb, "xres2", 2, (128, 4, T), F32)
                fgs = [(f0, min(8, FC - f0)) for f0 in range(0, FC, 8)]
                for ti in range(ntiles):
                    kb.dma("sp", hT[:], cm(H[:, cs(ti)]), reads=bH[ti], writes=[hbuf])
                    for dg in range(8):
                        pss = [psr.next() for _ in range(4)]
                        xt_, xb_ = xres.next()
                        kb.dma("sp", xt_[:], cm(X1[dg * 512:(dg + 1) * 512, cs(ti)]), reads=[bX1[ti]], writes=[xb_])
                        for gi, (f0, nf) in enumerate(fgs):
                            wt, wb = wd.next()
                            kb.dma("pool", wt[:, 0:nf, :],
                                   w_down[f0 * 128:(f0 + nf) * 128, dg * 512:(dg + 1) * 512].rearrange(
                                       "(fc p) d -> p fc d", p=128), writes=[wb])
                            fns = []
                            for fi in range(nf):
                                for j in range(4):
                                    f = f0 + fi
                                    fns.append(lambda pe, fi=fi, j=j, f=f: pe.matmul(
                                        pss[j][0][:], wt[:, fi, j * 128:(j + 1) * 128], hT[:, f, :],
                                        start=(f == 0), stop=(f == FC - 1)))
                            kb.mm(fns, reads=[wb, hbuf], writes=[p[1] for p in pss])
                        for j in range(4):
                            ch = dg * 4 + j
                            kb.op("dve", lambda v: v.scalar_tensor_tensor(out=yT[:, ch, :], in0=xt_[:, j, :], scalar=ALPHA,
                                                                          in1=pss[j][0][:], op0=ALU.mult, op1=ALU.add),
                                  reads=[pss[j][1], xb_], writes=[ybufs[ch]])
                    norm.stats(yT, ybufs, KC, D, LN_EPS, True)
                    for c in range(KC):
                        norm.apply(yT[:, c, :], ybufs[c], c, lnt[:, 2 * KC:3 * KC], lnt[:, 3 * KC:4 * KC], True,
                                   [(yT[:, c, :], ybufs[c])])
                    kb.dma("sp", cm(x2T[:, cs(ti)]), yT[:], reads=ybufs, writes=[bX2[ti]], track=track)
                barrier(kb)
        if first and do_mla:
            build_mla_pre(kb, nc, psr, ntiles, x2T if do_ffn else X1, bX2 if do_ffn else bX1, w_inc, w_uq, w_ukv, nrt, tok0,
                          QnT, QrT, KnT, VT, KrT, track)
        kb.st = st0
        kb.finish(track)
    return nc


def build_mla_pre(kb, nc, psr, ntiles, xsrc, bxs, w_inc, w_uq, w_ukv, nrt, tok0, QnT, QrT, KnT, VT, KrT, track):
    def cs(ti):
        return slice(ti * T, (ti + 1) * T)

    with ExitStack() as st:
        kb.st = st
        norm = Norm(kb, psr, "m")
        perm = kb.sb("perm64", [128, 64], BF16)
        bperm = make_perm(kb, perm, 64)
        rope = Rope(kb, 64, tok0)
        xbr = Ring(kb, "x2b", 1, (128, KC, T), BF16)
        wr = Ring(kb, "wm", 2, (128, KC, 256), BF16)
        wq = Ring(kb, "wq", 2, (128, 12, 384), BF16)
        wk = Ring(kb, "wk", 2, (128, 4, 512), BF16)
        cq = kb.sb("cq", [128, 17, T], F32)
        cbufs = [Buf() for _ in range(17)]
        cqn = kb.sb("cqn", [128, 16, T], BF16)
        nbuf = Buf()
        obr = Ring(kb, "mo", 4, (128, T), BF16)
        qbr = Ring(kb, "mq", 2, (64, T), BF16)
        t1r = Ring(kb, "mt1", 2, (64, T), F32)
        t2r = Ring(kb, "mt2", 2, (64, T), F32)

        def rope64(src_ps, src_buf, dst_ap):
            qb, qbuf = qbr.next()
            kb.op("act", lambda a: a.copy(out=qb[:], in_=src_ps), reads=[src_buf], writes=[qbuf])
            pw, pwb = psr.next()
            kb.mm([lambda pe: pe.matmul(pw[0:64, :], perm[0:64, :], qb[:], start=True, stop=True)],
                  reads=[bperm, qbuf], writes=[pwb])
            t1, t1b = t1r.next()
            t2, t2b = t2r.next()
            ob, obb = obr.next()
            kb.op("dve", lambda v: v.tensor_tensor(out=t1[:], in0=qb[:], in1=rope.C[0:64, :], op=ALU.mult),
                  reads=[qbuf, rope.btab], writes=[t1b])
            kb.op("dve", lambda v: v.tensor_tensor(out=t2[:], in0=pw[0:64, :], in1=rope.Sg[0:64, :], op=ALU.mult),
                  reads=[pwb, rope.btab], writes=[t2b])
            kb.op("pool", lambda g: g.tensor_tensor(out=ob[0:64, :], in0=t1[:], in1=t2[:], op=ALU.add),
                  reads=[t1b, t2b], writes=[obb])
            kb.dma("sp", dst_ap, ob[0:64, :], reads=[obb], track=track)

        for ti in range(ntiles):
            xt, xbuf = xbr.next()
            kb.dma("pool", xt[:], xsrc[:, cs(ti)].rearrange("(c p) t -> p c t", p=128), reads=[bxs[ti]], writes=[xbuf])
            rope.tile(ti * T)

            def evac_in(c0, m, pt, pbuf):
                ch = c0 // 128
                kb.op("act", lambda a: a.copy(out=cq[0:m, ch, :], in_=pt[0:m, :]), reads=[pbuf], writes=[cbufs[ch]])
            groups = [[(g * 256, 128), (g * 256 + 128, 128)] for g in range(8)] + [[(2048, 64)]]
            linear(kb, xt, xbuf, KC, w_inc, groups, wr, psr, evac_in)
            for lo, n, nf in ((0, 12, Q_LORA), (12, 4, KV_LORA)):
                norm.stats(cq[:, lo:lo + n, :], cbufs[lo:lo + n], n, nf, RMS_EPS, False)
                for c in range(n):
                    norm.apply(cq[:, lo + c, :], cbufs[lo + c], lo + c, nrt, None, False, [(cqn[:, lo + c, :], nbuf)])
            rope64(cq[0:64, 16, :], cbufs[16], KrT[:, cs(ti)])
            for hp in range(MLA_H // 2):
                wt, wb = wq.next()
                kb.dma("pool", wt[:], w_uq[:, hp * 384:(hp + 1) * 384].rearrange("(kc p) f -> p kc f", p=128),
                       writes=[wb])
                for hh in range(2):
                    h = hp * 2 + hh
                    pn, pnb = psr.next()
                    pr, prb = psr.next()
                    fns = []
                    for k in range(12):
                        fns.append(lambda pe, k=k: pe.matmul(pn[:], wt[:, k, hh * 192:hh * 192 + 128], cqn[:, k, :],
                                                             start=(k == 0), stop=(k == 11)))
                        fns.append(lambda pe, k=k: pe.matmul(pr[0:64, :], wt[:, k, hh * 192 + 128:hh * 192 + 192],
                                                             cqn[:, k, :], start=(k == 0), stop=(k == 11)))
                    kb.mm(fns, reads=[wb, nbuf], writes=[pnb, prb])
                    ob, obb = obr.next()
                    kb.op("act", lambda a: a.copy(out=ob[:], in_=pn[:]), reads=[pnb], writes=[obb])
                    kb.dma("sp", QnT[h * 128:(h + 1) * 128, cs(ti)], ob[:], reads=[obb], track=track)
                    rope64(pr[0:64, :], prb, QrT[h * 64:(h + 1) * 64, cs(ti)])
            for hp in range(MLA_H // 2):
                wt, wb = wk.next()
                kb.dma("pool", wt[:], w_ukv[:, hp * 512:(hp + 1) * 512].rearrange("(kc p) f -> p kc f", p=128),
                       writes=[wb])
                pss = [psr.next() for _ in range(4)]
                fns = []
                for k in range(4):
                    for j in range(4):
                        fns.append(lambda pe, k=k, j=j: pe.matmul(pss[j][0][:], wt[:, k, j * 128:(j + 1) * 128],
                                                                  cqn[:, 12 + k, :], start=(k == 0), stop=(k == 3)))
                kb.mm(fns, reads=[wb, nbuf], writes=[p[1] for p in pss])
                for j in range(4):
                    h = hp * 2 + j // 2
                    dst = (KnT if j % 2 == 0 else VT)[h * 128:(h + 1) * 128, cs(ti)]
                    ob, obb = obr.next()
                    kb.op("act", lambda a: a.copy(out=ob[:], in_=pss[j][0][:]), reads=[pss[j][1]], writes=[obb])
                    kb.dma("sp", dst, ob[:], reads=[obb], track=track)
        barrier(kb)


HPC = MLA_H // NCORES


def build_l4(nq_tiles=S // T, heads=HPC):
    nc = bass.Bass("TRN2", target_bir_lowering=False)
    Qn = nc.dram_tensor("Qn", [HPC * 128, S], BF16, kind="ExternalInput").ap()
    Qr = nc.dram_tensor("Qr", [HPC * 64, S], BF16, kind="ExternalInput").ap()
    Kn = nc.dram_tensor("Kn", [HPC * 128, S], BF16, kind="ExternalInput").ap()
    Kr = nc.dram_tensor("Kr", [64, S], BF16, kind="ExternalInput").ap()
    Vt = nc.dram_tensor("Vt", [HPC * 128, S // 128, 128], BF16, kind="ExternalInput").ap()
    OT = nc.dram_tensor("OT", [HPC * 128, S], BF16, kind="ExternalOutput").ap()
    scale = 192.0 ** -0.5
    track = {}
    with ExitStack() as st:
        kb = KB(nc, st)
        sps = Ring(kb, "sps", 3, (128, 512), F32, psum=True)
        aps = Ring(kb, "aps", 4, (128, 512), F32, psum=True)
        ones = kb.sb("ones", [128, 128], BF16)
        bones = Buf()
        kb.op("pool", lambda g: g.memset(ones[:], 1.0), writes=[bones])
        masks = kb.sb("masks", [128, 4, T], BF16)
        bmask = Buf()
        kb.op("pool", lambda g: g.memset(masks[:], 1.0), writes=[bmask])
        for j in range(4):
            kb.op("pool", lambda g: g.affine_select(out=masks[:, j, :], in_=masks[:, j, :], pattern=[[1, T]],
                                                    compare_op=ALU.is_ge, fill=0.0, base=-128 * j,
                                                    channel_multiplier=-1), writes=[bmask])
        krt = kb.sb("kr", [64, S], BF16)
        bkr = Buf()
        kb.dma("sp", krt[:], Kr, writes=[bkr])
        knt = kb.sb("kn", [128, S], BF16)
        bkn = Buf()
        vt = kb.sb("vt", [128, S // 128, 128], BF16)
        bv = Buf()
        qnr = Ring(kb, "qn", 2, (128, T), BF16)
        qrr = Ring(kb, "qr", 2, (64, T), BF16)
        ptr = Ring(kb, "pt", 4, (128, T), BF16)
        rcr = Ring(kb, "rc", 2, (128, T), F32)
        obr = Ring(kb, "ob", 2, (128, T), BF16)
        for h in range(heads):
            kb.dma("sp", knt[:], Kn[h * 128:(h + 1) * 128, :], writes=[bkn])
            kb.dma("sp", vt[:], Vt[h * 128:(h + 1) * 128, :, :], writes=[bv])
            for i in range(nq_tiles):
                qn, qnb = qnr.next()
                qr, qrb = qrr.next()
                kb.dma("sp", qn[:], Qn[h * 128:(h + 1) * 128, i * T:(i + 1) * T], writes=[qnb])
                kb.dma("sp", qr[:], Qr[h * 64:(h + 1) * 64, i * T:(i + 1) * T], writes=[qrb])
                ao, aob = aps.next()
                ad, adb = aps.next()
                nkb = 4 * i + 4

                def S_(kbi):
                    sp_, spb = sps.next()
                    kb.mm([lambda pe: pe.matmul(sp_[:], knt[:, kbi * 128:(kbi + 1) * 128], qn[:], start=True, stop=False),
                           lambda pe: pe.matmul(sp_[:], krt[:, kbi * 128:(kbi + 1) * 128], qr[:], start=False, stop=True)],
                          reads=[bkn, bkr, qnb, qrb], writes=[spb])
                    return sp_, spb
                cur = S_(0)
                for kbi in range(nkb):
                    nxt = S_(kbi + 1) if kbi + 1 < nkb else None
                    sp_, spb = cur
                    pt, ptb = ptr.next()
                    kb.op("act", lambda a: a.activation(out=pt[:], in_=sp_[:], func=AF.Exp, scale=scale),
                          reads=[spb], writes=[ptb])
                    if kbi >= 4 * i:
                        j = kbi - 4 * i
                        kb.op("dve", lambda v: v.tensor_tensor(out=pt[:], in0=pt[:], in1=masks[:, j, :], op=ALU.mult),
                              reads=[bmask], writes=[ptb])
                    kb.mm([lambda pe: pe.matmul(ao[:], vt[:, kbi, :], pt[:], start=(kbi == 0), stop=(kbi == nkb - 1)),
                           lambda pe: pe.matmul(ad[:], ones[:], pt[:], start=(kbi == 0), stop=(kbi == nkb - 1))],
                          reads=[bv, bones, ptb], writes=[aob, adb])
                    cur = nxt
                rc, rcb = rcr.next()
                ob, obb = obr.next()
                kb.op("dve", lambda v: v.reciprocal(out=rc[:], in_=ad[:]), reads=[adb], writes=[rcb])
                kb.op("dve", lambda v: v.tensor_tensor(out=ob[:], in0=ao[:], in1=rc[:], op=ALU.mult),
                      reads=[aob, rcb], writes=[obb])
                kb.dma("sp", OT[h * 128:(h + 1) * 128, i * T:(i + 1) * T], ob[:], reads=[obb], track=track)
        kb.finish(track)
    return nc


DILS = (1, 4, 16)
SPAN = 2048


def build_l2(nspans=S // SPAN, heads=2):
    nc = bass.Bass("TRN2", target_bir_lowering=False)
    q = nc.dram_tensor("q", [2 * 128, S], BF16, kind="ExternalInput").ap()
    k = nc.dram_tensor("k", [2 * 128, S], BF16, kind="ExternalInput").ap()
    Vd = nc.dram_tensor("Vd", [2 * 3 * 128, S // 128, 128], BF16, kind="ExternalInput").ap()
    aT = nc.dram_tensor("aT", [2 * 128, S], BF16, kind="ExternalOutput").ap()
    scale = 128.0 ** -0.5
    track = {}
    with ExitStack() as st:
        kb = KB(nc, st)
        sps = Ring(kb, "sps", 3, (128, 512), F32, psum=True)
        ops_ = Ring(kb, "ops", 3, (128, 512), F32, psum=True)
        ones = kb.sb("ones", [128, 128], BF16)
        bones = Buf()
        kb.op("pool", lambda g: g.memset(ones[:], 1.0), writes=[bones])
        mask = kb.sb("mask", [128, 256], BF16)
        bmask = Buf()
        kb.op("pool", lambda g: g.memset(mask[:], 1.0), writes=[bmask])
        kb.op("pool", lambda g: g.affine_select(out=mask[:, 0:128], in_=mask[:, 0:128], pattern=[[1, 128]],
                                                compare_op=ALU.is_ge, fill=0.0, base=0, channel_multiplier=-1),
              writes=[bmask])
        kb.op("pool", lambda g: g.affine_select(out=mask[:, 128:256], in_=mask[:, 128:256], pattern=[[-1, 128]],
                                                compare_op=ALU.is_ge, fill=0.0, base=0, channel_multiplier=1),
              writes=[bmask])
        qt = kb.sb("qt", [128, S], BF16)
        kt = kb.sb("kt", [128, S], BF16)
        vts = [kb.sb("v%d" % i, [128, S // 128, 128], BF16) for i in range(3)]
        bq, bk, bvs = Buf(), Buf(), [Buf() for _ in range(3)]
        acc = kb.sb("acc", [128, 2, SPAN], F32)
        bacc = Buf()
        ptr = Ring(kb, "pt", 4, (128, 256), BF16)
        obr = Ring(kb, "ob", 1, (128, SPAN), BF16)
        for h in range(heads):
            kb.dma("sp", qt[:], q[h * 128:(h + 1) * 128, :], writes=[bq])
            kb.dma("sp", kt[:], k[h * 128:(h + 1) * 128, :], writes=[bk])
            for di in range(3):
                r0 = (h * 3 + di) * 128
                kb.dma("sp", vts[di][:], Vd[r0:r0 + 128, :, :], writes=[bvs[di]])
            for sp in range(nspans):
                for di, d in enumerate(DILS):
                    nb = 128 // d
                    for r in range(d):
                        for n in range(SPAN * sp // (128 * d), SPAN * (sp + 1) // (128 * d)):
                            b = r * nb + n
                            t0 = r + 128 * d * n
                            cq = slice(t0, t0 + 127 * d + 1, d)
                            cp = slice(t0 - 128 * d, t0 - d + 1, d)
                            has_prev = n > 0
                            w = 256 if has_prev else 128
                            sp_, spb = sps.next()
                            fns = [lambda pe: pe.matmul(sp_[:, 0:128], kt[:, cq], qt[:, cq], start=True, stop=True)]
                            if has_prev:
                                fns.append(lambda pe: pe.matmul(sp_[:, 128:256], kt[:, cp], qt[:, cq], start=True, stop=True))
                            kb.mm(fns, reads=[bq, bk], writes=[spb])
                            pt, ptb = ptr.next()
                            kb.op("act", lambda a: a.activation(out=pt[:, 0:w], in_=sp_[:, 0:w], func=AF.Exp, scale=scale),
                                  reads=[spb], writes=[ptb])
                            kb.op("dve", lambda v: v.tensor_tensor(out=pt[:, 0:w], in0=pt[:, 0:w], in1=mask[:, 0:w],
                                                                   op=ALU.mult), reads=[bmask], writes=[ptb])
                            op_, opb = ops_.next()
                            fns = [lambda pe: pe.matmul(op_[:, 0:128], vts[di][:, b, :], pt[:, 0:128], start=True,
                                                        stop=not has_prev)]
                            if has_prev:
                                fns.append(lambda pe: pe.matmul(op_[:, 0:128], vts[di][:, b - 1, :], pt[:, 128:256],
                                                                start=False, stop=True))
                            fns.append(lambda pe: pe.matmul(op_[:, 128:256], ones[:], pt[:, 0:128], start=True,
                                                            stop=not has_prev))
                            if has_prev:
                                fns.append(lambda pe: pe.matmul(op_[:, 128:256], ones[:], pt[:, 128:256], start=False,
                                                                stop=True))
                            kb.mm(fns, reads=[bvs[di], bones, ptb], writes=[opb])
                            lc = slice(t0 - SPAN * sp, t0 - SPAN * sp + 127 * d + 1, d)
                            for a_ in range(2):
                                src = op_[:, a_ * 128:(a_ + 1) * 128]
                                if di == 0:
                                    kb.op("dve", lambda v: v.tensor_copy(out=acc[:, a_, lc], in_=src),
                                          reads=[opb], writes=[bacc])
                                else:
                                    kb.op("dve", lambda v: v.tensor_tensor(out=acc[:, a_, lc], in0=acc[:, a_, lc], in1=src,
                                                                           op=ALU.add), reads=[opb], writes=[bacc])
                ob, obb = obr.next()
                kb.op("dve", lambda v: v.reciprocal(out=acc[:, 1, :], in_=acc[:, 1, :]), writes=[bacc])
                kb.op("dve", lambda v: v.tensor_tensor(out=ob[:], in0=acc[:, 0, :], in1=acc[:, 1, :], op=ALU.mult),
                      reads=[bacc], writes=[obb])
                kb.dma("sp", aT[h * 128:(h + 1) * 128, sp * SPAN:(sp + 1) * SPAN], ob[:], reads=[obb], track=track)
        kb.finish(track)
    return nc


def _cols(v, n):
    return np.ascontiguousarray(np.asarray(v, np.float32).reshape(n, 128).T)


def _vd_layout(Vh):
    outs = []
    for d in DILS:
        nb = 128 // d
        a = Vh.reshape(nb, 128, d, 128).transpose(1, 2, 0, 3).reshape(128, d * nb, 128)
        outs.append(a)
    return np.stack(outs)


def run_l2(o1):
    nc = build_l2()
    in_maps = []
    for c in range(NCORES):
        rows = slice(c * 256, (c + 1) * 256)
        vd = np.stack([_vd_layout(np.ascontiguousarray(o1["vT"][(2 * c + hh) * 128:(2 * c + hh + 1) * 128, :].T))
                       for hh in range(2)])
        in_maps.append({"q": np.ascontiguousarray(o1["qT"][rows]), "k": np.ascontiguousarray(o1["kT"][rows]),
                        "Vd": np.ascontiguousarray(vd).reshape(2 * 3 * 128, S // 128, 128)})
    res = run_bass_kernel_spmd(nc, in_maps, core_ids=list(range(NCORES)))
    return np.concatenate([r["aT"] for r in res.results], axis=0)


def run_chain(first, inT, xTfull, w_out, w_gate, w_up, w_down, lnp, extra=None):
    nc = build_chain(first)
    in_maps = []
    for c in range(NTC):
        cs = slice(c * TOK, (c + 1) * TOK)
        m = {"inT": np.ascontiguousarray(inT[:, cs]), "xT": np.ascontiguousarray(xTfull[:, cs]),
             "w_out": w_out, "w_gate": w_gate, "w_up": w_up, "w_down": w_down, "lnp": lnp}
        if first:
            uT = extra["uT"]
            halo = uT[:, c * TOK - 2:c * TOK] if c > 0 else np.zeros((A_WIDTH, 2), np.float32)
            m.update(uTh=np.ascontiguousarray(np.concatenate([halo, uT[:, cs]], axis=1)),
                     gbT=np.ascontiguousarray(extra["gbT"][:, cs]), cwp=extra["cwp"],
                     tok0=np.full((128, 1), c * TOK, np.float32), w_inc=extra["w_inc"], w_uq=extra["w_uq"],
                     w_ukv=extra["w_ukv"], nrm=extra["nrm"])
        in_maps.append(m)
    res = run_bass_kernel_spmd(nc, in_maps, core_ids=list(range(NTC)))
    names = ["x2T"] + (["QnT", "QrT", "KnT", "VT", "KrT"] if first else [])
    return {n: np.concatenate([r[n] for r in res.results], axis=1) for n in names}


def run_l4(o3):
    nc = build_l4()
    in_maps = []
    for c in range(NCORES):
        vt = []
        for hh in range(HPC):
            h = c * HPC + hh
            Vh = np.ascontiguousarray(o3["VT"][h * 128:(h + 1) * 128, :].T)
            vt.append(Vh.reshape(S // 128, 128, 128).transpose(1, 0, 2))
        in_maps.append({"Qn": np.ascontiguousarray(o3["QnT"][c * 512:(c + 1) * 512]),
                        "Qr": np.ascontiguousarray(o3["QrT"][c * 256:(c + 1) * 256]),
                        "Kn": np.ascontiguousarray(o3["KnT"][c * 512:(c + 1) * 512]),
                        "Kr": np.ascontiguousarray(o3["KrT"]),
                        "Vt": np.ascontiguousarray(np.stack(vt)).reshape(HPC * 128, S // 128, 128)})
    res = run_bass_kernel_spmd(nc, in_maps, core_ids=list(range(NCORES)))
    return np.concatenate([r["OT"] for r in res.results], axis=0)


def kernel(x, w_in_a, conv_w, w_out_a, w_in_c, q_norm, kv_norm, w_uq, w_ukv, w_out_c,
           ln1_g, ln1_b, w_gate, w_up, w_down, ln2_g, ln2_b):
    f32 = lambda a: np.asarray(a, dtype=np.float32)
    xT = np.ascontiguousarray(f32(x).reshape(S, D).T)
    o1 = run_l1(xT, f32(w_in_a)[0])
    aT = run_l2(o1)
    ln1_g, ln1_b, ln2_g, ln2_b = f32(ln1_g), f32(ln1_b), f32(ln2_g), f32(ln2_b)
    lnp0 = np.ascontiguousarray(np.concatenate([_cols(ln1_g[0], KC), _cols(ln1_b[0], KC), _cols(ln2_g[0], KC),
                                                _cols(ln2_b[0], KC)], axis=1))
    lnp1 = np.ascontiguousarray(np.concatenate([_cols(ln1_g[1], KC), _cols(ln1_b[1], KC), _cols(ln2_g[1], KC),
                                                _cols(ln2_b[1], KC)], axis=1))
    cw = f32(conv_w)[0]
    extra = {"uT": o1["uT"], "gbT": o1["gbT"],
             "cwp": np.ascontiguousarray(cw.T.reshape(16, 128, 3).transpose(1, 0, 2).reshape(128, 48)),
             "w_inc": f32(w_in_c)[0], "w_uq": f32(w_uq)[0], "w_ukv": f32(w_ukv)[0],
             "nrm": np.ascontiguousarray(np.concatenate([_cols(f32(q_norm)[0], 12), _cols(f32(kv_norm)[0], 4)], axis=1))}
    w_gate, w_up, w_down = f32(w_gate), f32(w_up), f32(w_down)
    o3 = run_chain(True, aT, xT, f32(w_out_a)[0], w_gate[0], w_up[0], w_down[0], lnp0, extra)
    del o1, extra, aT
    OT = run_l4(o3)
    o5 = run_chain(False, OT, o3["x2T"], f32(w_out_c)[0], w_gate[1], w_up[1], w_down[1], lnp1)
    return np.ascontiguousarray(o5["x2T"].T).reshape(1, S, D).astype(np.float32)
```

```python
import math
from contextlib import ExitStack

import numpy as np
import ml_dtypes

import concourse.bass as bass
import concourse.mybir as mybir
from concourse.bass_utils import run_bass_kernel_spmd

F32 = mybir.dt.float32
BF16 = mybir.dt.bfloat16
AF = mybir.ActivationFunctionType
ALU = mybir.AluOpType
NPBF = ml_dtypes.bfloat16

NCORES = 8
D = 4096
S = 16384
NTC = 4
TOK = S // NTC
T = 512
KC = D // 128
A_WIDTH = 2048
FFN = 11008
FC = FFN // 128
ALPHA = 4.0 ** 0.25
LN_EPS = 1e-5
RMS_EPS = 1e-6
THETA = 10000.0
Q_LORA, KV_LORA, QK_ROPE = 1536, 512, 64
MLA_IN = Q_LORA + KV_LORA + QK_ROPE
MLA_H = 32
PI = math.pi


class Buf:
    __slots__ = ("w", "r")

    def __init__(self):
        self.w = None
        self.r = {}


class KB:
    NDS = 48

    def __init__(self, nc, st):
        self.nc = nc
        self.st = st
        self.sems = []
        self.E = {}
        for name, eng in (("pe", nc.tensor), ("act", nc.scalar), ("dve", nc.vector),
                          ("pool", nc.gpsimd), ("sp", nc.sync)):
            sem = st.enter_context(nc.semaphore("s_" + name))
            self.sems.append(sem)
            self.E[name] = dict(eng=eng, sid=len(self.sems) - 1, cnt=0, waited={})
        self.dsid = []
        for i in range(self.NDS):
            sem = st.enter_context(nc.semaphore("d%d" % i))
            self.sems.append(sem)
            self.dsid.append(len(self.sems) - 1)
        self.dval = [0] * self.NDS
        self.dnext = 0
        self.nbuf = 0

    def sb(self, name, shape, dt):
        return self.st.enter_context(self.nc.sbuf_tensor(name, shape, dt))

    def ps(self, name, shape=(128, 512), dt=F32):
        return self.st.enter_context(self.nc.psum_tensor(name, list(shape), dt))

    def wait(self, en, ev, is_dma=False):
        if ev is None:
            return
        sid, v = ev
        e = self.E[en]
        if sid == e["sid"] and en == "pe":
            return
        if e["waited"].get(sid, 0) >= v:
            return
        e["eng"].wait_ge(self.sems[sid], v)
        e["waited"][sid] = v

    def _deps(self, en, reads, writes, is_dma=False):
        for b in reads:
            self.wait(en, b.w, is_dma)
        for b in writes:
            self.wait(en, b.w, is_dma)
            for sid, v in b.r.items():
                self.wait(en, (sid, v), is_dma)

    def _mark(self, ev, reads, writes):
        sid, v = ev
        for b in reads:
            if b.r.get(sid, 0) < v:
                b.r[sid] = v
        for b in writes:
            b.w = ev
            b.r = {}

    def op(self, en, fn, reads=(), writes=()):
        self._deps(en, reads, writes)
        e = self.E[en]
        ins = fn(e["eng"])
        e["cnt"] += 1
        ins.then_inc(self.sems[e["sid"]], 1)
        self._mark((e["sid"], e["cnt"]), reads, writes)

    def mm(self, fns, reads=(), writes=()):
        self._deps("pe", reads, writes)
        e = self.E["pe"]
        ins = None
        for fn in fns:
            ins = fn(e["eng"])
        e["cnt"] += 1
        ins.then_inc(self.sems[e["sid"]], 1)
        self._mark((e["sid"], e["cnt"]), reads, writes)

    def dma(self, qn, out, in_, reads=(), writes=(), track=None):
        self._deps(qn, reads, writes, is_dma=True)
        i = self.dnext
        self.dnext = (i + 1) % self.NDS
        if self.dval[i] > 0:
            self.wait(qn, (self.dsid[i], self.dval[i]), True)
        self.dval[i] += 16
        self.E[qn]["eng"].dma_start(out=out, in_=in_).then_inc(self.sems[self.dsid[i]], 16)
        self._mark((self.dsid[i], self.dval[i]), reads, writes)
        if track is not None:
            track[self.dsid[i]] = self.dval[i]

    def finish(self, track):
        for sid, v in track.items():
            self.wait("sp", (sid, v), True)


class Ring:
    def __init__(self, kb, name, n, shape, dt, psum=False):
        self.t = []
        for i in range(n):
            t = kb.ps(name + str(i), shape, dt) if psum else kb.sb(name + str(i), list(shape), dt)
            self.t.append((t, Buf()))
        self.i = 0

    def next(self):
        r = self.t[self.i]
        self.i = (self.i + 1) % len(self.t)
        return r


def w_src(w, c0, ncols, kc=None):
    return w[:, c0:c0 + ncols].rearrange("(kc p) f -> p kc f", p=128)


def make_perm(kb, perm, n):
    h = n // 2
    b = Buf()
    kb.op("pool", lambda g: g.memset(perm[:], 0.0), writes=[b])
    kb.op("pool", lambda g: g.affine_select(out=perm[0:n, 0:h], in_=perm[0:n, 0:h], pattern=[[-1, h]],
                                             compare_op=ALU.not_equal, fill=1.0, base=-h,
                                             channel_multiplier=1), writes=[b])
    kb.op("pool", lambda g: g.affine_select(out=perm[0:n, h:n], in_=perm[0:n, h:n], pattern=[[-1, h]],
                                             compare_op=ALU.not_equal, fill=1.0, base=0,
                                             channel_multiplier=1), writes=[b])
    return b


class Rope:
    def __init__(self, kb, n, tok0_ap):
        self.kb, self.n = kb, n
        h = n // 2
        self.h = h
        I32 = mybir.dt.int32
        self.jidx = kb.sb("rp_j%d" % n, [128, T], F32)
        self.pidx = kb.sb("rp_p%d" % n, [128, 1], F32)
        self.pm = kb.sb("rp_pm%d" % n, [128, 1], F32)
        self.invf = kb.sb("rp_f%d" % n, [128, 1], F32)
        self.tokb = kb.sb("rp_tb%d" % n, [128, 1], F32)
        self.tok0 = kb.sb("rp_t0%d" % n, [128, 1], F32)
        self.r = kb.sb("rp_r%d" % n, [128, T], F32)
        self.ni = kb.sb("rp_ni%d" % n, [128, T], I32)
        self.nf = kb.sb("rp_nf%d" % n, [128, T], F32)
        self.f = kb.sb("rp_fr%d" % n, [128, T], F32)
        self.C = kb.sb("rp_c%d" % n, [128, T], F32)
        self.Sg = kb.sb("rp_s%d" % n, [128, T], F32)
        self.bconst = Buf()
        self.btab = Buf()
        self.btmp = Buf()
        kb.dma("sp", self.tok0[:], tok0_ap, writes=[self.bconst])
        kb.op("pool", lambda g: g.iota(self.jidx[:], [[1, T]], base=0, channel_multiplier=0,
                                       allow_small_or_imprecise_dtypes=True), writes=[self.bconst])
        kb.op("pool", lambda g: g.iota(self.pidx[:], [[0, 1]], base=0, channel_multiplier=1,
                                       allow_small_or_imprecise_dtypes=True), writes=[self.bconst])
        kb.op("dve", lambda v: v.tensor_single_scalar(out=self.pm[:], in_=self.pidx[:], scalar=float(h),
                                                      op=ALU.is_ge), reads=[self.bconst], writes=[self.btmp])
        kb.op("dve", lambda v: v.scalar_tensor_tensor(out=self.pidx[:], in0=self.pm[:], scalar=-float(h),
                                                      in1=self.pidx[:], op0=ALU.mult, op1=ALU.add),
              writes=[self.bconst])
        kb.op("act", lambda a: a.activation(out=self.invf[:], in_=self.pidx[:], func=AF.Exp,
                                            scale=-math.log(THETA) / h), reads=[self.bconst], writes=[self.btmp])
        kb.op("dve", lambda v: v.tensor_scalar_mul(out=self.invf[:], in0=self.invf[:], scalar1=1.0 / (2 * PI)),
              reads=[self.btmp], writes=[self.bconst])

    def _frac(self):
        kb = self.kb
        kb.op("dve", lambda v: v.tensor_copy(out=self.ni[:], in_=self.r[:]), writes=[self.btmp])
        kb.op("dve", lambda v: v.tensor_copy(out=self.nf[:], in_=self.ni[:]), writes=[self.btmp])
        kb.op("dve", lambda v: v.tensor_tensor(out=self.f[:], in0=self.r[:], in1=self.nf[:], op=ALU.subtract),
              writes=[self.btmp])

    def tile(self, t0):
        kb, n, h = self.kb, self.n, self.h
        kb.op("dve", lambda v: v.tensor_scalar_add(out=self.tokb[:], in0=self.tok0[:], scalar1=float(t0)),
              reads=[self.bconst], writes=[self.btmp])
        kb.op("dve", lambda v: v.tensor_scalar(out=self.r[:], in0=self.jidx[:], scalar1=self.tokb[:, 0:1],
                                               scalar2=self.invf[:, 0:1], op0=ALU.add, op1=ALU.mult),
              reads=[self.bconst], writes=[self.btmp])
        self._frac()
        kb.op("act", lambda a: a.activation(out=self.Sg[0:h, :], in_=self.f[0:h, :], func=AF.Sin, scale=-2 * PI),
              reads=[self.btmp], writes=[self.btab])
        kb.op("act", lambda a: a.activation(out=self.Sg[h:n, :], in_=self.f[h:n, :], func=AF.Sin, scale=2 * PI),
              reads=[self.btmp], writes=[self.btab])
        kb.op("dve", lambda v: v.tensor_scalar_add(out=self.r[:], in0=self.r[:], scalar1=0.25), writes=[self.btmp])
        self._frac()
        kb.op("act", lambda a: a.activation(out=self.C[0:n, :], in_=self.f[0:n, :], func=AF.Sin, scale=2 * PI),
              reads=[self.btmp], writes=[self.btab])


def build_l1(ntiles=TOK // T, blocks_sel=None):
    nc = bass.Bass("TRN2", target_bir_lowering=False)
    xT = nc.dram_tensor("xT", [D, TOK], F32, kind="ExternalInput").ap()
    w = nc.dram_tensor("w_in", [D, 3 * A_WIDTH + 3 * A_WIDTH], F32, kind="ExternalInput").ap()
    tok0 = nc.dram_tensor("tok0", [128, 1], F32, kind="ExternalInput").ap()
    qT = nc.dram_tensor("qT", [A_WIDTH, TOK], BF16, kind="ExternalOutput").ap()
    kT = nc.dram_tensor("kT", [A_WIDTH, TOK], BF16, kind="ExternalOutput").ap()
    vT = nc.dram_tensor("vT", [A_WIDTH, TOK], BF16, kind="ExternalOutput").ap()
    gbT = nc.dram_tensor("gbT", [A_WIDTH, TOK], F32, kind="ExternalOutput").ap()
    uT = nc.dram_tensor("uT", [A_WIDTH, TOK], F32, kind="ExternalOutput").ap()
    with ExitStack() as st:
        kb = KB(nc, st)
        perm = kb.sb("perm", [128, 128], BF16)
        bperm = make_perm(kb, perm, 128)
        rope = Rope(kb, 128, tok0)
        xring = Ring(kb, "xb", 2, (128, KC, T), BF16)
        wring = Ring(kb, "wb", 2, (128, KC, 512), BF16)
        psr = Ring(kb, "ps", 6, (128, 512), F32, psum=True)
        pswr = Ring(kb, "psw", 2, (128, 512), F32, psum=True)
        qbr = Ring(kb, "qb", 2, (128, T), BF16)
        t1r = Ring(kb, "t1", 2, (128, T), F32)
        t2r = Ring(kb, "t2", 2, (128, T), F32)
        obr = Ring(kb, "ob", 4, (128, T), BF16)
        ofr = Ring(kb, "of", 4, (128, T), F32)
        gcb = [(kb.sb("gc%d" % i, [128, T], F32), Buf()) for i in range(4)]
        bout = {}
        blocks = []
        for typ, base in (("q", 0), ("k", 2048), ("v", 4096), ("gb", 6144)):
            for i in range(4):
                blocks.append((typ, base + 512 * i, i))
        for i in range(4):
            blocks.append(("gc", 8192 + 512 * i, i))
            blocks.append(("hin", 10240 + 512 * i, i))
        outs = {"q": qT, "k": kT, "v": vT, "gb": gbT, "hin": uT}

        def load_x(ti):
            xt, xbuf = xring.next()
            kb.dma("pool", xt[:], xT[:, ti * T:(ti + 1) * T].rearrange("(kc p) t -> p kc t", p=128),
                   writes=[xbuf])
            return xt, xbuf

        nxt = load_x(0)
        if blocks_sel is not None:
            blocks = [blocks[i] for i in blocks_sel]
        for ti in range(ntiles):
            xt, xbuf = nxt
            rope.tile(ti * T)
            for bi, (typ, c0, i) in enumerate(blocks):
                wt, wbuf = wring.next()
                kb.dma("pool", wt[:], w_src(w, c0, 512), writes=[wbuf])
                if bi == min(4, len(blocks) - 1) and ti + 1 < ntiles:
                    nxt = load_x(ti + 1)
                for half in range(2):
                    pss = [psr.next() for _ in range(2)]
                    fns = []
                    for k in range(KC):
                        for j in range(2):
                            cc = (half * 2 + j) * 128
                            fns.append(lambda pe, k=k, j=j, cc=cc: pe.matmul(
                                pss[j][0][:], wt[:, k, cc:cc + 128], xt[:, k, :],
                                start=(k == 0), stop=(k == KC - 1)))
                    kb.mm(fns, reads=[wbuf, xbuf], writes=[pss[0][1], pss[1][1]])
                    for j in range(2):
                        pt, pbuf = pss[j]
                        ch = half * 2 + j
                        row0 = (c0 % 2048) + ch * 128
                        cols = slice(ti * T, (ti + 1) * T)
                        if typ in ("q", "k"):
                            qb, qbuf = qbr.next()
                            kb.op("act", lambda a: a.copy(out=qb[:], in_=pt[:]), reads=[pbuf], writes=[qbuf])
                            pw, pwbuf = pswr.next()
                            kb.mm([lambda pe: pe.matmul(pw[:], perm[:], qb[:], start=True, stop=True)],
                                  reads=[bperm, qbuf], writes=[pwbuf])
                            t1, t1b = t1r.next()
                            t2, t2b = t2r.next()
                            ob, obb = obr.next()
                            kb.op("dve", lambda v: v.tensor_tensor(out=t1[:], in0=qb[:], in1=rope.C[:], op=ALU.mult),
                                  reads=[qbuf, rope.btab], writes=[t1b])
                            kb.op("dve", lambda v: v.tensor_tensor(out=t2[:], in0=pw[:], in1=rope.Sg[:], op=ALU.mult),
                                  reads=[pwbuf, rope.btab], writes=[t2b])
                            kb.op("pool", lambda g: g.tensor_tensor(out=ob[:], in0=t1[:], in1=t2[:], op=ALU.add),
                                  reads=[t1b, t2b], writes=[obb])
                            kb.dma("sp", outs[typ][row0:row0 + 128, cols], ob[:], reads=[obb], track=bout)
                        elif typ == "v":
                            ob, obb = obr.next()
                            kb.op("act", lambda a: a.copy(out=ob[:], in_=pt[:]), reads=[pbuf], writes=[obb])
                            kb.dma("sp", vT[row0:row0 + 128, cols], ob[:], reads=[obb], track=bout)
                        elif typ == "gb":
                            of, ofb = ofr.next()
                            kb.op("act", lambda a: a.copy(out=of[:], in_=pt[:]), reads=[pbuf], writes=[ofb])
                            kb.dma("sp", gbT[row0:row0 + 128, cols], of[:], reads=[ofb], track=bout)
                        elif typ == "gc":
                            gt, gbuf = gcb[ch]
                            kb.op("act", lambda a: a.copy(out=gt[:], in_=pt[:]), reads=[pbuf], writes=[gbuf])
                        else:
                            gt, gbuf = gcb[ch]
                            of, ofb = ofr.next()
                            kb.op("dve", lambda v: v.tensor_tensor(out=of[:], in0=gt[:], in1=pt[:], op=ALU.mult),
                                  reads=[pbuf, gbuf], writes=[ofb])
                            kb.dma("sp", uT[row0:row0 + 128, cols], of[:], reads=[ofb], track=bout)
        kb.finish(bout)
    return nc


def run_l1(xTfull, w_in_a):
    nc = build_l1()
    in_maps = []
    for c in range(NTC):
        in_maps.append({"xT": np.ascontiguousarray(xTfull[:, c * TOK:(c + 1) * TOK]),
                        "w_in": w_in_a, "tok0": np.full((128, 1), c * TOK, np.float32)})
    res = run_bass_kernel_spmd(nc, in_maps, core_ids=list(range(NTC)))
    out = {}
    for name in ("qT", "kT", "vT", "gbT", "uT"):
        out[name] = np.concatenate([r[name] for r in res.results], axis=1)
    return out


def barrier(kb):
    evs = [(e["sid"], e["cnt"]) for e in kb.E.values() if e["cnt"] > 0]
    evs += [(kb.dsid[i], kb.dval[i]) for i in range(kb.NDS) if kb.dval[i] > 0]
    for en in kb.E:
        for ev in evs:
            kb.wait(en, ev, is_dma=True)


def linear(kb, xt, xbuf, kcn, w, groups, wring, psr, evac, wq="pool"):
    for grp in groups:
        g0 = grp[0][0]
        gw = grp[-1][0] + grp[-1][1] - g0
        wt, wbuf = wring.next()
        kb.dma(wq, wt[:, 0:kcn, 0:gw], w[:, g0:g0 + gw].rearrange("(kc p) f -> p kc f", p=128), writes=[wbuf])
        pss = [psr.next() for _ in grp]
        fns = []
        for k in range(kcn):
            for j, (c0, m) in enumerate(grp):
                fns.append(lambda pe, k=k, j=j, c0=c0, m=m: pe.matmul(
                    pss[j][0][0:m, :], wt[:, k, c0 - g0:c0 - g0 + m], xt[:, k, :],
                    start=(k == 0), stop=(k == kcn - 1)))
        kb.mm(fns, reads=[wbuf, xbuf], writes=[p[1] for p in pss])
        for j, (c0, m) in enumerate(grp):
            evac(c0, m, pss[j][0], pss[j][1])


class Norm:
    def __init__(self, kb, psr, tag):
        self.kb, self.psr = kb, psr
        self.ones = kb.sb("n1_" + tag, [128, 128], F32)
        self.bones = Buf()
        kb.op("pool", lambda g: g.memset(self.ones[:], 1.0), writes=[self.bones])
        self.sq = Ring(kb, "nsq_" + tag, 2, (128, T), F32)
        self.mean = kb.sb("nmu_" + tag, [128, T], F32)
        self.rstd = kb.sb("nrs_" + tag, [128, T], F32)
        self.tmp = Ring(kb, "ntp_" + tag, 2, (128, T), F32)
        self.bstat = Buf()

    def stats(self, yT, ybufs, nch, nfeat, eps, center):
        kb = self.kb
        p2, p2b = self.psr.next()
        if center:
            p1, p1b = self.psr.next()
            kb.mm([lambda pe, c=c: pe.matmul(p1[:], self.ones[:], yT[:, c, :], start=(c == 0), stop=(c == nch - 1))
                   for c in range(nch)], reads=[self.bones] + ybufs, writes=[p1b])
        for c in range(nch):
            sq, sqb = self.sq.next()
            kb.op("act", lambda a: a.activation(out=sq[:], in_=yT[:, c, :], func=AF.Square),
                  reads=[ybufs[c]], writes=[sqb])
            kb.mm([lambda pe: pe.matmul(p2[:], self.ones[:], sq[:], start=(c == 0), stop=(c == nch - 1))],
                  reads=[self.bones, sqb], writes=[p2b])
        inv = 1.0 / nfeat
        if center:
            kb.op("dve", lambda v: v.tensor_scalar_mul(out=self.mean[:], in0=p1[:], scalar1=inv),
                  reads=[p1b], writes=[self.bstat])
            tm, tmb = self.tmp.next()
            kb.op("dve", lambda v: v.tensor_tensor(out=tm[:], in0=self.mean[:], in1=self.mean[:], op=ALU.mult),
                  reads=[self.bstat], writes=[tmb])
            kb.op("dve", lambda v: v.scalar_tensor_tensor(out=self.rstd[:], in0=p2[:], scalar=inv, in1=tm[:],
                                                          op0=ALU.mult, op1=ALU.subtract),
                  reads=[p2b, tmb], writes=[self.bstat])
            kb.op("dve", lambda v: v.tensor_scalar_add(out=self.rstd[:], in0=self.rstd[:], scalar1=eps),
                  writes=[self.bstat])
        else:
            kb.op("dve", lambda v: v.tensor_scalar(out=self.rstd[:], in0=p2[:], scalar1=inv, scalar2=eps,
                                                   op0=ALU.mult, op1=ALU.add), reads=[p2b], writes=[self.bstat])
        kb.op("act", lambda a: a.activation(out=self.rstd[:], in_=self.rstd[:], func=AF.Sqrt),
              reads=[self.bstat], writes=[self.bstat])
        kb.op("dve", lambda v: v.reciprocal(out=self.rstd[:], in_=self.rstd[:]), reads=[self.bstat],
              writes=[self.bstat])

    def apply(self, src, srcbuf, c, gam, bet, center, outs):
        kb = self.kb
        tm, tmb = self.tmp.next()
        if center:
            kb.op("dve", lambda v: v.tensor_tensor(out=tm[:], in0=src, in1=self.mean[:], op=ALU.subtract),
                  reads=[srcbuf, self.bstat], writes=[tmb])
            kb.op("dve", lambda v: v.tensor_tensor(out=tm[:], in0=tm[:], in1=self.rstd[:], op=ALU.mult),
                  reads=[self.bstat], writes=[tmb])
        else:
            kb.op("dve", lambda v: v.tensor_tensor(out=tm[:], in0=src, in1=self.rstd[:], op=ALU.mult),
                  reads=[srcbuf, self.bstat], writes=[tmb])
        for ap, buf in outs:
            if bet is not None:
                kb.op("act", lambda a: a.activation(out=ap, in_=tm[:], func=AF.Identity, scale=gam[:, c:c + 1],
                                                    bias=bet[:, c:c + 1]), reads=[tmb], writes=[buf])
            else:
                kb.op("act", lambda a: a.activation(out=ap, in_=tm[:], func=AF.Identity, scale=gam[:, c:c + 1]),
                      reads=[tmb], writes=[buf])


def load_cols(kb, name, ap, n):
    t = kb.sb(name, [128, n], F32)
    b = Buf()
    kb.dma("sp", t[:], ap, writes=[b])
    return t, b


def build_chain(first, ntiles=TOK // T, do_ffn=True, do_mla=True):
    nc = bass.Bass("TRN2", target_bir_lowering=False)
    dt_in = nc.dram_tensor
    inT = dt_in("inT", [D if not first else A_WIDTH, TOK], BF16, kind="ExternalInput").ap()
    xT = dt_in("xT", [D, TOK], F32, kind="ExternalInput").ap()
    w_out = dt_in("w_out", [D, D], F32, kind="ExternalInput").ap()
    w_gate = dt_in("w_gate", [D, FFN], F32, kind="ExternalInput").ap()
    w_up = dt_in("w_up", [D, FFN], F32, kind="ExternalInput").ap()
    w_down = dt_in("w_down", [FFN, D], F32, kind="ExternalInput").ap()
    lnp = dt_in("lnp", [128, 4 * KC], F32, kind="ExternalInput").ap()
    if first:
        uTh = dt_in("uTh", [A_WIDTH, TOK + 2], F32, kind="ExternalInput").ap()
        gbT = dt_in("gbT", [A_WIDTH, TOK], F32, kind="ExternalInput").ap()
        cwp = dt_in("cwp", [128, 16 * 3], F32, kind="ExternalInput").ap()
        tok0 = dt_in("tok0", [128, 1], F32, kind="ExternalInput").ap()
        w_inc = dt_in("w_inc", [D, MLA_IN], F32, kind="ExternalInput").ap()
        w_uq = dt_in("w_uq", [Q_LORA, MLA_H * 192], F32, kind="ExternalInput").ap()
        w_ukv = dt_in("w_ukv", [KV_LORA, MLA_H * 256], F32, kind="ExternalInput").ap()
        nrm = dt_in("nrm", [128, 16], F32, kind="ExternalInput").ap()
        QnT = dt_in("QnT", [MLA_H * 128, TOK], BF16, kind="ExternalOutput").ap()
        QrT = dt_in("QrT", [MLA_H * 64, TOK], BF16, kind="ExternalOutput").ap()
        KnT = dt_in("KnT", [MLA_H * 128, TOK], BF16, kind="ExternalOutput").ap()
        VT = dt_in("VT", [MLA_H * 128, TOK], BF16, kind="ExternalOutput").ap()
        KrT = dt_in("KrT", [64, TOK], BF16, kind="ExternalOutput").ap()
    x2T = dt_in("x2T", [D, TOK], F32, kind="ExternalOutput").ap()
    X1 = dt_in("X1", [D, TOK], F32, kind="Internal").ap()
    X1B = dt_in("X1B", [D, TOK], BF16, kind="Internal").ap()
    H = dt_in("H", [FFN, TOK], BF16, kind="Internal").ap()
    track = {}
    bX1 = [Buf() for _ in range(ntiles)]
    bX1B = [Buf() for _ in range(ntiles)]
    bH = [[Buf() for _ in range(FC)] for _ in range(ntiles)]
    bX2 = [Buf() for _ in range(ntiles)]

    def cs(ti):
        return slice(ti * T, (ti + 1) * T)

    def cm(ap):
        return ap.rearrange("(c p) t -> p c t", p=128)

    with ExitStack() as st0:
        kb = KB(nc, st0)
        psr = Ring(kb, "ps", 8, (128, 512), F32, psum=True)
        lnt, _ = load_cols(kb, "lnp_sb", lnp, 4 * KC)
        if first:
            cwt, _ = load_cols(kb, "cwp_sb", cwp, 48)
            nrt, _ = load_cols(kb, "nrm_sb", nrm, 16)
        barrier(kb)

        with ExitStack() as st:
            kb.st = st
            norm = Norm(kb, psr, "a")
            inb = Ring(kb, "inb", 1, (128, KC, T), BF16)
            wring = Ring(kb, "w1", 2, (128, KC, 256), BF16)
            yT = kb.sb("yT", [128, KC, T], F32)
            ybufs = [Buf() for _ in range(KC)]
            xres = Ring(kb, "xres", 2, (128, 2, T), F32)
            if first:
                ur = Ring(kb, "ur", 2, (128, T + 2), F32)
                gr = Ring(kb, "gr", 2, (128, T), F32)
                ct = Ring(kb, "ct", 2, (128, T), F32)
            for ti in range(ntiles):
                it, ibuf = inb.next()
                if first:
                    kb.dma("sp", it[:, 0:16, :], cm(inT[:, cs(ti)]), writes=[ibuf])
                    for c in range(16):
                        ut, ub = ur.next()
                        gt, gb_ = gr.next()
                        tt, tb = ct.next()
                        kb.dma("sp", ut[:], uTh[c * 128:(c + 1) * 128, ti * T:ti * T + T + 2], writes=[ub])
                        kb.dma("sp", gt[:], gbT[c * 128:(c + 1) * 128, cs(ti)], writes=[gb_])
                        kb.op("dve", lambda v: v.tensor_scalar_mul(out=tt[:], in0=ut[:, 2:T + 2],
                                                                   scalar1=cwt[:, 3 * c + 2:3 * c + 3]),
                              reads=[ub], writes=[tb])
                        kb.op("dve", lambda v: v.scalar_tensor_tensor(out=tt[:], in0=ut[:, 1:T + 1],
                                                                      scalar=cwt[:, 3 * c + 1:3 * c + 2], in1=tt[:],
                                                                      op0=ALU.mult, op1=ALU.add),
                              reads=[ub], writes=[tb])
                        kb.op("dve", lambda v: v.scalar_tensor_tensor(out=tt[:], in0=ut[:, 0:T],
                                                                      scalar=cwt[:, 3 * c:3 * c + 1], in1=tt[:],
                                                                      op0=ALU.mult, op1=ALU.add),
                              reads=[ub], writes=[tb])
                        kb.op("dve", lambda v: v.tensor_tensor(out=it[:, 16 + c, :], in0=tt[:], in1=gt[:], op=ALU.mult),
                              reads=[tb, gb_], writes=[ibuf])
                else:
                    kb.dma("sp", it[:], cm(inT[:, cs(ti)]), writes=[ibuf])

                def evac1(c0, m, pt, pbuf):
                    ch = c0 // 128
                    if ch % 2 == 0:
                        evac1.x = xres.next()
                        kb.dma("sp", evac1.x[0][:], cm(xT[ch * 128:(ch + 2) * 128, cs(ti)]), writes=[evac1.x[1]])
                    xt_, xb_ = evac1.x
                    kb.op("dve", lambda v: v.scalar_tensor_tensor(out=yT[:, ch, :], in0=xt_[:, ch % 2, :], scalar=ALPHA,
                                                                  in1=pt[:], op0=ALU.mult, op1=ALU.add),
                          reads=[pbuf, xb_], writes=[ybufs[ch]])
                groups = [[(g * 256, 128), (g * 256 + 128, 128)] for g in range(16)]
                linear(kb, it, ibuf, KC, w_out, groups, wring, psr, evac1)
                norm.stats(yT, ybufs, KC, D, LN_EPS, True)
                for c in range(KC):
                    norm.apply(yT[:, c, :], ybufs[c], c, lnt[:, 0:KC], lnt[:, KC:2 * KC], True,
                               [(yT[:, c, :], ybufs[c])])
                    kb.op("act", lambda a: a.copy(out=it[:, c, :], in_=yT[:, c, :]), reads=[ybufs[c]], writes=[ibuf])
                kb.dma("sp", cm(X1[:, cs(ti)]), yT[:], reads=ybufs, writes=[bX1[ti]])
                kb.dma("sp", cm(X1B[:, cs(ti)]), it[:], reads=[ibuf], writes=[bX1B[ti]])
            barrier(kb)

        if do_ffn:
            with ExitStack() as st:
                kb.st = st
                xbr = Ring(kb, "x1b", 2, (128, KC, T), BF16)
                wg = Ring(kb, "wg", 2, (128, KC, 256), BF16)
                wu = Ring(kb, "wu", 2, (128, KC, 256), BF16)
                sgr = Ring(kb, "sg", 2, (128, T), F32)
                hbr = Ring(kb, "hb", 4, (128, T), BF16)
                for ti in range(ntiles):
                    xt, xbuf = xbr.next()
                    kb.dma("sp", xt[:], cm(X1B[:, cs(ti)]), reads=[bX1B[ti]], writes=[xbuf])
                    for g in range(FC // 2):
                        wgt, wgb = wg.next()
                        wut, wub = wu.next()
                        kb.dma("pool", wgt[:], w_src(w_gate, g * 256, 256), writes=[wgb])
                        kb.dma("pool", wut[:], w_src(w_up, g * 256, 256), writes=[wub])
                        pg = [psr.next() for _ in range(2)]
                        pu = [psr.next() for _ in range(2)]
                        fns = []
                        for k in range(KC):
                            for j in range(2):
                                fns.append(lambda pe, k=k, j=j: pe.matmul(pg[j][0][:], wgt[:, k, j * 128:(j + 1) * 128],
                                                                          xt[:, k, :], start=(k == 0), stop=(k == KC - 1)))
                                fns.append(lambda pe, k=k, j=j: pe.matmul(pu[j][0][:], wut[:, k, j * 128:(j + 1) * 128],
                                                                          xt[:, k, :], start=(k == 0), stop=(k == KC - 1)))
                        kb.mm(fns, reads=[wgb, wub, xbuf], writes=[p[1] for p in pg + pu])
                        for j in range(2):
                            fch = g * 2 + j
                            sg, sgb = sgr.next()
                            hb, hbb = hbr.next()
                            kb.op("act", lambda a: a.activation(out=sg[:], in_=pg[j][0][:], func=AF.Silu),
                                  reads=[pg[j][1]], writes=[sgb])
                            kb.op("dve", lambda v: v.tensor_tensor(out=hb[:], in0=sg[:], in1=pu[j][0][:], op=ALU.mult),
                                  reads=[sgb, pu[j][1]], writes=[hbb])
                            kb.dma("sp", H[fch * 128:(fch + 1) * 128, cs(ti)], hb[:], reads=[hbb], writes=[bH[ti][fch]])
                barrier(kb)

            with ExitStack() as st:
                kb.st = st
                norm = Norm(kb, psr, "b")
                hT = kb.sb("hT", [128, FC, T], BF16)
                hbuf = Buf()
                wd = Ring(kb, "wd", 2, (128, 8, 512), BF16)
                yT = kb.sb("yT2", [128, KC, T], F32)
                ybufs = [Buf() for _ in range(KC)]
                xres = Ring(kb, "xres2", 2, (128, 4, T), F32)
                fgs = [(f0, min(8, FC - f0)) for f0 in range(0, FC, 8)]
                for ti in range(ntiles):
                    kb.dma("sp", hT[:], cm(H[:, cs(ti)]), reads=bH[ti], writes=[hbuf])
                    for dg in range(8):
                        pss = [psr.next() for _ in range(4)]
                        xt_, xb_ = xres.next()
                        kb.dma("sp", xt_[:], cm(X1[dg * 512:(dg + 1) * 512, cs(ti)]), reads=[bX1[ti]], writes=[xb_])
                        for gi, (f0, nf) in enumerate(fgs):
                            wt, wb = wd.next()
                            kb.dma("pool", wt[:, 0:nf, :],
                                   w_down[f0 * 128:(f0 + nf) * 128, dg * 512:(dg + 1) * 512].rearrange(
                                       "(fc p) d -> p fc d", p=128), writes=[wb])
                            fns = []
                            for fi in range(nf):
                                for j in range(4):
                                    f = f0 + fi
                                    fns.append(lambda pe, fi=fi, j=j, f=f: pe.matmul(
                                        pss[j][0][:], wt[:, fi, j * 128:(j + 1) * 128], hT[:, f, :],
                                        start=(f == 0), stop=(f == FC - 1)))
                            kb.mm(fns, reads=[wb, hbuf], writes=[p[1] for p in pss])
                        for j in range(4):
                            ch = dg * 4 + j
                            kb.op("dve", lambda v: v.scalar_tensor_tensor(out=yT[:, ch, :], in0=xt_[:, j, :], scalar=ALPHA,
                                                                          in1=pss[j][0][:], op0=ALU.mult, op1=ALU.add),
                                  reads=[pss[j][1], xb_], writes=[ybufs[ch]])
                    norm.stats(yT, ybufs, KC, D, LN_EPS, True)
                    for c in range(KC):
                        norm.apply(yT[:, c, :], ybufs[c], c, lnt[:, 2 * KC:3 * KC], lnt[:, 3 * KC:4 * KC], True,
                                   [(yT[:, c, :], ybufs[c])])
                    kb.dma("sp", cm(x2T[:, cs(ti)]), yT[:], reads=ybufs, writes=[bX2[ti]], track=track)
                barrier(kb)
        if first and do_mla:
            build_mla_pre(kb, nc, psr, ntiles, x2T if do_ffn else X1, bX2 if do_ffn else bX1, w_inc, w_uq, w_ukv, nrt, tok0,
                          QnT, QrT, KnT, VT, KrT, track)
        kb.st = st0
        kb.finish(track)
    return nc


def build_mla_pre(kb, nc, psr, ntiles, xsrc, bxs, w_inc, w_uq, w_ukv, nrt, tok0, QnT, QrT, KnT, VT, KrT, track):
    def cs(ti):
        return slice(ti * T, (ti + 1) * T)

    with ExitStack() as st:
        kb.st = st
        norm = Norm(kb, psr, "m")
        perm = kb.sb("perm64", [128, 64], BF16)
        bperm = make_perm(kb, perm, 64)
        rope = Rope(kb, 64, tok0)
        xbr = Ring(kb, "x2b", 1, (128, KC, T), BF16)
        wr = Ring(kb, "wm", 2, (128, KC, 256), BF16)
        wq = Ring(kb, "wq", 2, (128, 12, 384), BF16)
        wk = Ring(kb, "wk", 2, (128, 4, 512), BF16)
        cq = kb.sb("cq", [128, 17, T], F32)
        cbufs = [Buf() for _ in range(17)]
        cqn = kb.sb("cqn", [128, 16, T], BF16)
        nbuf = Buf()
        obr = Ring(kb, "mo", 4, (128, T), BF16)
        qbr = Ring(kb, "mq", 2, (64, T), BF16)
        t1r = Ring(kb, "mt1", 2, (64, T), F32)
        t2r = Ring(kb, "mt2", 2, (64, T), F32)

        def rope64(src_ps, src_buf, dst_ap):
            qb, qbuf = qbr.next()
            kb.op("act", lambda a: a.copy(out=qb[:], in_=src_ps), reads=[src_buf], writes=[qbuf])
            pw, pwb = psr.next()
            kb.mm([lambda pe: pe.matmul(pw[0:64, :], perm[0:64, :], qb[:], start=True, stop=True)],
                  reads=[bperm, qbuf], writes=[pwb])
            t1, t1b = t1r.next()
            t2, t2b = t2r.next()
            ob, obb = obr.next()
            kb.op("dve", lambda v: v.tensor_tensor(out=t1[:], in0=qb[:], in1=rope.C[0:64, :], op=ALU.mult),
                  reads=[qbuf, rope.btab], writes=[t1b])
            kb.op("dve", lambda v: v.tensor_tensor(out=t2[:], in0=pw[0:64, :], in1=rope.Sg[0:64, :], op=ALU.mult),
                  reads=[pwb, rope.btab], writes=[t2b])
            kb.op("pool", lambda g: g.tensor_tensor(out=ob[0:64, :], in0=t1[:], in1=t2[:], op=ALU.add),
                  reads=[t1b, t2b], writes=[obb])
            kb.dma("sp", dst_ap, ob[0:64, :], reads=[obb], track=track)

        for ti in range(ntiles):
            xt, xbuf = xbr.next()
            kb.dma("pool", xt[:], xsrc[:, cs(ti)].rearrange("(c p) t -> p c t", p=128), reads=[bxs[ti]], writes=[xbuf])
            rope.tile(ti * T)

            def evac_in(c0, m, pt, pbuf):
                ch = c0 // 128
                kb.op("act", lambda a: a.copy(out=cq[0:m, ch, :], in_=pt[0:m, :]), reads=[pbuf], writes=[cbufs[ch]])
            groups = [[(g * 256, 128), (g * 256 + 128, 128)] for g in range(8)] + [[(2048, 64)]]
            linear(kb, xt, xbuf, KC, w_inc, groups, wr, psr, evac_in)
            for lo, n, nf in ((0, 12, Q_LORA), (12, 4, KV_LORA)):
                norm.stats(cq[:, lo:lo + n, :], cbufs[lo:lo + n], n, nf, RMS_EPS, False)
                for c in range(n):
                    norm.apply(cq[:, lo + c, :], cbufs[lo + c], lo + c, nrt, None, False, [(cqn[:, lo + c, :], nbuf)])
            rope64(cq[0:64, 16, :], cbufs[16], KrT[:, cs(ti)])
            for hp in range(MLA_H // 2):
                wt, wb = wq.next()
                kb.dma("pool", wt[:], w_uq[:, hp * 384:(hp + 1) * 384].rearrange("(kc p) f -> p kc f", p=128),
                       writes=[wb])
                for hh in range(2):
                    h = hp * 2 + hh
                    pn, pnb = psr.next()
                    pr, prb = psr.next()
                    fns = []
                    for k in range(12):
                        fns.append(lambda pe, k=k: pe.matmul(pn[:], wt[:, k, hh * 192:hh * 192 + 128], cqn[:, k, :],
                                                             start=(k == 0), stop=(k == 11)))
                        fns.append(lambda pe, k=k: pe.matmul(pr[0:64, :], wt[:, k, hh * 192 + 128:hh * 192 + 192],
                                                             cqn[:, k, :], start=(k == 0), stop=(k == 11)))
                    kb.mm(fns, reads=[wb, nbuf], writes=[pnb, prb])
                    ob, obb = obr.next()
                    kb.op("act", lambda a: a.copy(out=ob[:], in_=pn[:]), reads=[pnb], writes=[obb])
                    kb.dma("sp", QnT[h * 128:(h + 1) * 128, cs(ti)], ob[:], reads=[obb], track=track)
                    rope64(pr[0:64, :], prb, QrT[h * 64:(h + 1) * 64, cs(ti)])
            for hp in range(MLA_H // 2):
                wt, wb = wk.next()
                kb.dma("pool", wt[:], w_ukv[:, hp * 512:(hp + 1) * 512].rearrange("(kc p) f -> p kc f", p=128),
                       writes=[wb])
                pss = [psr.next() for _ in range(4)]
                fns = []
                for k in range(4):
                    for j in range(4):
                        fns.append(lambda pe, k=k, j=j: pe.matmul(pss[j][0][:], wt[:, k, j * 128:(j + 1) * 128],
                                                                  cqn[:, 12 + k, :], start=(k == 0), stop=(k == 3)))
                kb.mm(fns, reads=[wb, nbuf], writes=[p[1] for p in pss])
                for j in range(4):
                    h = hp * 2 + j // 2
                    dst = (KnT if j % 2 == 0 else VT)[h * 128:(h + 1) * 128, cs(ti)]
                    ob, obb = obr.next()
                    kb.op("act", lambda a: a.copy(out=ob[:], in_=pss[j][0][:]), reads=[pss[j][1]], writes=[obb])
                    kb.dma("sp", dst, ob[:], reads=[obb], track=track)
        barrier(kb)


HPC = MLA_H // NCORES


def build_l4(nq_tiles=S // T, heads=HPC):
    nc = bass.Bass("TRN2", target_bir_lowering=False)
    Qn = nc.dram_tensor("Qn", [HPC * 128, S], BF16, kind="ExternalInput").ap()
    Qr = nc.dram_tensor("Qr", [HPC * 64, S], BF16, kind="ExternalInput").ap()
    Kn = nc.dram_tensor("Kn", [HPC * 128, S], BF16, kind="ExternalInput").ap()
    Kr = nc.dram_tensor("Kr", [64, S], BF16, kind="ExternalInput").ap()
    Vt = nc.dram_tensor("Vt", [HPC * 128, S // 128, 128], BF16, kind="ExternalInput").ap()
    OT = nc.dram_tensor("OT", [HPC * 128, S], BF16, kind="ExternalOutput").ap()
    scale = 192.0 ** -0.5
    track = {}
    with ExitStack() as st:
        kb = KB(nc, st)
        sps = Ring(kb, "sps", 3, (128, 512), F32, psum=True)
        aps = Ring(kb, "aps", 4, (128, 512), F32, psum=True)
        ones = kb.sb("ones", [128, 128], BF16)
        bones = Buf()
        kb.op("pool", lambda g: g.memset(ones[:], 1.0), writes=[bones])
        masks = kb.sb("masks", [128, 4, T], BF16)
        bmask = Buf()
        kb.op("pool", lambda g: g.memset(masks[:], 1.0), writes=[bmask])
        for j in range(4):
            kb.op("pool", lambda g: g.affine_select(out=masks[:, j, :], in_=masks[:, j, :], pattern=[[1, T]],
                                                    compare_op=ALU.is_ge, fill=0.0, base=-128 * j,
                                                    channel_multiplier=-1), writes=[bmask])
        krt = kb.sb("kr", [64, S], BF16)
        bkr = Buf()
        kb.dma("sp", krt[:], Kr, writes=[bkr])
        knt = kb.sb("kn", [128, S], BF16)
        bkn = Buf()
        vt = kb.sb("vt", [128, S // 128, 128], BF16)
        bv = Buf()
        qnr = Ring(kb, "qn", 2, (128, T), BF16)
        qrr = Ring(kb, "qr", 2, (64, T), BF16)
        ptr = Ring(kb, "pt", 4, (128, T), BF16)
        rcr = Ring(kb, "rc", 2, (128, T), F32)
        obr = Ring(kb, "ob", 2, (128, T), BF16)
        for h in range(heads):
            kb.dma("sp", knt[:], Kn[h * 128:(h + 1) * 128, :], writes=[bkn])
            kb.dma("sp", vt[:], Vt[h * 128:(h + 1) * 128, :, :], writes=[bv])
            for i in range(nq_tiles):
                qn, qnb = qnr.next()
                qr, qrb = qrr.next()
                kb.dma("sp", qn[:], Qn[h * 128:(h + 1) * 128, i * T:(i + 1) * T], writes=[qnb])
                kb.dma("sp", qr[:], Qr[h * 64:(h + 1) * 64, i * T:(i + 1) * T], writes=[qrb])
                ao, aob = aps.next()
                ad, adb = aps.next()
                nkb = 4 * i + 4

                def S_(kbi):
                    sp_, spb = sps.next()
                    kb.mm([lambda pe: pe.matmul(sp_[:], knt[:, kbi * 128:(kbi + 1) * 128], qn[:], start=True, stop=False),
                           lambda pe: pe.matmul(sp_[:], krt[:, kbi * 128:(kbi + 1) * 128], qr[:], start=False, stop=True)],
                          reads=[bkn, bkr, qnb, qrb], writes=[spb])
                    return sp_, spb
                cur = S_(0)
                for kbi in range(nkb):
                    nxt = S_(kbi + 1) if kbi + 1 < nkb else None
                    sp_, spb = cur
                    pt, ptb = ptr.next()
                    kb.op("act", lambda a: a.activation(out=pt[:], in_=sp_[:], func=AF.Exp, scale=scale),
                          reads=[spb], writes=[ptb])
                    if kbi >= 4 * i:
                        j = kbi - 4 * i
                        kb.op("dve", lambda v: v.tensor_tensor(out=pt[:], in0=pt[:], in1=masks[:, j, :], op=ALU.mult),
                              reads=[bmask], writes=[ptb])
                    kb.mm([lambda pe: pe.matmul(ao[:], vt[:, kbi, :], pt[:], start=(kbi == 0), stop=(kbi == nkb - 1)),
                           lambda pe: pe.matmul(ad[:], ones[:], pt[:], start=(kbi == 0), stop=(kbi == nkb - 1))],
                          reads=[bv, bones, ptb], writes=[aob, adb])
                    cur = nxt
                rc, rcb = rcr.next()
                ob, obb = obr.next()
                kb.op("dve", lambda v: v.reciprocal(out=rc[:], in_=ad[:]), reads=[adb], writes=[rcb])
                kb.op("dve", lambda v: v.tensor_tensor(out=ob[:], in0=ao[:], in1=rc[:], op=ALU.mult),
                      reads=[aob, rcb], writes=[obb])
                kb.dma("sp", OT[h * 128:(h + 1) * 128, i * T:(i + 1) * T], ob[:], reads=[obb], track=track)
        kb.finish(track)
    return nc


DILS = (1, 4, 16)
SPAN = 2048


def build_l2(nspans=S // SPAN, heads=2):
    nc = bass.Bass("TRN2", target_bir_lowering=False)
    q = nc.dram_tensor("q", [2 * 128, S], BF16, kind="ExternalInput").ap()
    k = nc.dram_tensor("k", [2 * 128, S], BF16, kind="ExternalInput").ap()
    Vd = nc.dram_tensor("Vd", [2 * 3 * 128, S // 128, 128], BF16, kind="ExternalInput").ap()
    aT = nc.dram_tensor("aT", [2 * 128, S], BF16, kind="ExternalOutput").ap()
    scale = 128.0 ** -0.5
    track = {}
    with ExitStack() as st:
        kb = KB(nc, st)
        sps = Ring(kb, "sps", 3, (128, 512), F32, psum=True)
        ops_ = Ring(kb, "ops", 3, (128, 512), F32, psum=True)
        ones = kb.sb("ones", [128, 128], BF16)
        bones = Buf()
        kb.op("pool", lambda g: g.memset(ones[:], 1.0), writes=[bones])
        mask = kb.sb("mask", [128, 256], BF16)
        bmask = Buf()
        kb.op("pool", lambda g: g.memset(mask[:], 1.0), writes=[bmask])
        kb.op("pool", lambda g: g.affine_select(out=mask[:, 0:128], in_=mask[:, 0:128], pattern=[[1, 128]],
                                                compare_op=ALU.is_ge, fill=0.0, base=0, channel_multiplier=-1),
              writes=[bmask])
        kb.op("pool", lambda g: g.affine_select(out=mask[:, 128:256], in_=mask[:, 128:256], pattern=[[-1, 128]],
                                                compare_op=ALU.is_ge, fill=0.0, base=0, channel_multiplier=1),
              writes=[bmask])
        qt = kb.sb("qt", [128, S], BF16)
        kt = kb.sb("kt", [128, S], BF16)
        vts = [kb.sb("v%d" % i, [128, S // 128, 128], BF16) for i in range(3)]
        bq, bk, bvs = Buf(), Buf(), [Buf() for _ in range(3)]
        acc = kb.sb("acc", [128, 2, SPAN], F32)
        bacc = Buf()
        ptr = Ring(kb, "pt", 4, (128, 256), BF16)
        obr = Ring(kb, "ob", 1, (128, SPAN), BF16)
        for h in range(heads):
            kb.dma("sp", qt[:], q[h * 128:(h + 1) * 128, :], writes=[bq])
            kb.dma("sp", kt[:], k[h * 128:(h + 1) * 128, :], writes=[bk])
            for di in range(3):
                r0 = (h * 3 + di) * 128
                kb.dma("sp", vts[di][:], Vd[r0:r0 + 128, :, :], writes=[bvs[di]])
            for sp in range(nspans):
                for di, d in enumerate(DILS):
                    nb = 128 // d
                    for r in range(d):
                        for n in range(SPAN * sp // (128 * d), SPAN * (sp + 1) // (128 * d)):
                            b = r * nb + n
                            t0 = r + 128 * d * n
                            cq = slice(t0, t0 + 127 * d + 1, d)
                            cp = slice(t0 - 128 * d, t0 - d + 1, d)
                            has_prev = n > 0
                            w = 256 if has_prev else 128
                            sp_, spb = sps.next()
                            fns = [lambda pe: pe.matmul(sp_[:, 0:128], kt[:, cq], qt[:, cq], start=True, stop=True)]
                            if has_prev:
                                fns.append(lambda pe: pe.matmul(sp_[:, 128:256], kt[:, cp], qt[:, cq], start=True, stop=True))
                            kb.mm(fns, reads=[bq, bk], writes=[spb])
                            pt, ptb = ptr.next()
                            kb.op("act", lambda a: a.activation(out=pt[:, 0:w], in_=sp_[:, 0:w], func=AF.Exp, scale=scale),
                                  reads=[spb], writes=[ptb])
                            kb.op("dve", lambda v: v.tensor_tensor(out=pt[:, 0:w], in0=pt[:, 0:w], in1=mask[:, 0:w],
                                                                   op=ALU.mult), reads=[bmask], writes=[ptb])
                            op_, opb = ops_.next()
                            fns = [lambda pe: pe.matmul(op_[:, 0:128], vts[di][:, b, :], pt[:, 0:128], start=True,
                                                        stop=not has_prev)]
                            if has_prev:
                                fns.append(lambda pe: pe.matmul(op_[:, 0:128], vts[di][:, b - 1, :], pt[:, 128:256],
                                                                start=False, stop=True))
                            fns.append(lambda pe: pe.matmul(op_[:, 128:256], ones[:], pt[:, 0:128], start=True,
                                                            stop=not has_prev))
                            if has_prev:
                                fns.append(lambda pe: pe.matmul(op_[:, 128:256], ones[:], pt[:, 128:256], start=False,
                                                                stop=True))
                            kb.mm(fns, reads=[bvs[di], bones, ptb], writes=[opb])
                            lc = slice(t0 - SPAN * sp, t0 - SPAN * sp + 127 * d + 1, d)
                            for a_ in range(2):
                                src = op_[:, a_ * 128:(a_ + 1) * 128]
                                if di == 0:
                                    kb.op("dve", lambda v: v.tensor_copy(out=acc[:, a_, lc], in_=src),
                                          reads=[opb], writes=[bacc])
                                else:
                                    kb.op("dve", lambda v: v.tensor_tensor(out=acc[:, a_, lc], in0=acc[:, a_, lc], in1=src,
                                                                           op=ALU.add), reads=[opb], writes=[bacc])
                ob, obb = obr.next()
                kb.op("dve", lambda v: v.reciprocal(out=acc[:, 1, :], in_=acc[:, 1, :]), writes=[bacc])
                kb.op("dve", lambda v: v.tensor_tensor(out=ob[:], in0=acc[:, 0, :], in1=acc[:, 1, :], op=ALU.mult),
                      reads=[bacc], writes=[obb])
                kb.dma("sp", aT[h * 128:(h + 1) * 128, sp * SPAN:(sp + 1) * SPAN], ob[:], reads=[obb], track=track)
        kb.finish(track)
    return nc


def _cols(v, n):
    return np.ascontiguousarray(np.asarray(v, np.float32).reshape(n, 128).T)


def _vd_layout(Vh):
    outs = []
    for d in DILS:
        nb = 128 // d
        a = Vh.reshape(nb, 128, d, 128).transpose(1, 2, 0, 3).reshape(128, d * nb, 128)
        outs.append(a)
    return np.stack(outs)


def run_l2(o1):
    nc = build_l2()
    in_maps = []
    for c in range(NCORES):
        rows = slice(c * 256, (c + 1) * 256)
        vd = np.stack([_vd_layout(np.ascontiguousarray(o1["vT"][(2 * c + hh) * 128:(2 * c + hh + 1) * 128, :].T))
                       for hh in range(2)])
        in_maps.append({"q": np.ascontiguousarray(o1["qT"][rows]), "k": np.ascontiguousarray(o1["kT"][rows]),
                        "Vd": np.ascontiguousarray(vd).reshape(2 * 3 * 128, S // 128, 128)})
    res = run_bass_kernel_spmd(nc, in_maps, core_ids=list(range(NCORES)))
    return np.concatenate([r["aT"] for r in res.results], axis=0)


def run_chain(first, inT, xTfull, w_out, w_gate, w_up, w_down, lnp, extra=None):
    nc = build_chain(first)
    in_maps = []
    for c in range(NTC):
        cs = slice(c * TOK, (c + 1) * TOK)
        m = {"inT": np.ascontiguousarray(inT[:, cs]), "xT": np.ascontiguousarray(xTfull[:, cs]),
             "w_out": w_out, "w_gate": w_gate, "w_up": w_up, "w_down": w_down, "lnp": lnp}
        if first:
            uT = extra["uT"]
            halo = uT[:, c * TOK - 2:c * TOK] if c > 0 else np.zeros((A_WIDTH, 2), np.float32)
            m.update(uTh=np.ascontiguousarray(np.concatenate([halo, uT[:, cs]], axis=1)),
                     gbT=np.ascontiguousarray(extra["gbT"][:, cs]), cwp=extra["cwp"],
                     tok0=np.full((128, 1), c * TOK, np.float32), w_inc=extra["w_inc"], w_uq=extra["w_uq"],
                     w_ukv=extra["w_ukv"], nrm=extra["nrm"])
        in_maps.append(m)
    res = run_bass_kernel_spmd(nc, in_maps, core_ids=list(range(NTC)))
    names = ["x2T"] + (["QnT", "QrT", "KnT", "VT", "KrT"] if first else [])
    return {n: np.concatenate([r[n] for r in res.results], axis=1) for n in names}


def run_l4(o3):
    nc = build_l4()
    in_maps = []
    for c in range(NCORES):
        vt = []
        for hh in range(HPC):
            h = c * HPC + hh
            Vh = np.ascontiguousarray(o3["VT"][h * 128:(h + 1) * 128, :].T)
            vt.append(Vh.reshape(S // 128, 128, 128).transpose(1, 0, 2))
        in_maps.append({"Qn": np.ascontiguousarray(o3["QnT"][c * 512:(c + 1) * 512]),
                        "Qr": np.ascontiguousarray(o3["QrT"][c * 256:(c + 1) * 256]),
                        "Kn": np.ascontiguousarray(o3["KnT"][c * 512:(c + 1) * 512]),
                        "Kr": np.ascontiguousarray(o3["KrT"]),
                        "Vt": np.ascontiguousarray(np.stack(vt)).reshape(HPC * 128, S // 128, 128)})
    res = run_bass_kernel_spmd(nc, in_maps, core_ids=list(range(NCORES)))
    return np.concatenate([r["OT"] for r in res.results], axis=0)


def kernel(x, w_in_a, conv_w, w_out_a, w_in_c, q_norm, kv_norm, w_uq, w_ukv, w_out_c,
           ln1_g, ln1_b, w_gate, w_up, w_down, ln2_g, ln2_b):
    f32 = lambda a: np.asarray(a, dtype=np.float32)
    xT = np.ascontiguousarray(f32(x).reshape(S, D).T)
    o1 = run_l1(xT, f32(w_in_a)[0])
    aT = run_l2(o1)
    ln1_g, ln1_b, ln2_g, ln2_b = f32(ln1_g), f32(ln1_b), f32(ln2_g), f32(ln2_b)
    lnp0 = np.ascontiguousarray(np.concatenate([_cols(ln1_g[0], KC), _cols(ln1_b[0], KC), _cols(ln2_g[0], KC),
                                                _cols(ln2_b[0], KC)], axis=1))
    lnp1 = np.ascontiguousarray(np.concatenate([_cols(ln1_g[1], KC), _cols(ln1_b[1], KC), _cols(ln2_g[1], KC),
                                                _cols(ln2_b[1], KC)], axis=1))
    cw = f32(conv_w)[0]
    extra = {"uT": o1["uT"], "gbT": o1["gbT"],
             "cwp": np.ascontiguousarray(cw.T.reshape(16, 128, 3).transpose(1, 0, 2).reshape(128, 48)),
             "w_inc": f32(w_in_c)[0], "w_uq": f32(w_uq)[0], "w_ukv": f32(w_ukv)[0],
             "nrm": np.ascontiguousarray(np.concatenate([_cols(f32(q_norm)[0], 12), _cols(f32(kv_norm)[0], 4)], axis=1))}
    w_gate, w_up, w_down = f32(w_gate), f32(w_up), f32(w_down)
    o3 = run_chain(True, aT, xT, f32(w_out_a)[0], w_gate[0], w_up[0], w_down[0], lnp0, extra)
    del o1, extra, aT
    OT = run_l4(o3)
    o5 = run_chain(False, OT, o3["x2T"], f32(w_out_c)[0], w_gate[1], w_up[1], w_down[1], lnp1)
    return np.ascontiguousarray(o5["x2T"].T).reshape(1, S, D).astype(np.float32)
```

```python
import math
from contextlib import ExitStack

import numpy as np
import ml_dtypes

import concourse.bass as bass
import concourse.mybir as mybir
from concourse.bass_utils import run_bass_kernel_spmd

F32 = mybir.dt.float32
BF16 = mybir.dt.bfloat16
AF = mybir.ActivationFunctionType
ALU = mybir.AluOpType
NPBF = ml_dtypes.bfloat16

NCORES = 8
D = 4096
S = 16384
NTC = 8
TOK = S // NTC
T = 512
KC = D // 128
A_WIDTH = 2048
FFN = 11008
FC = FFN // 128
ALPHA = 4.0 ** 0.25
LN_EPS = 1e-5
RMS_EPS = 1e-6
THETA = 10000.0
Q_LORA, KV_LORA, QK_ROPE = 1536, 512, 64
MLA_IN = Q_LORA + KV_LORA + QK_ROPE
MLA_H = 32
PI = math.pi


class Buf:
    __slots__ = ("w", "r")

    def __init__(self):
        self.w = None
        self.r = {}


class KB:
    NDS = 48

    def __init__(self, nc, st):
        self.nc = nc
        self.st = st
        self.sems = []
        self.E = {}
        for name, eng in (("pe", nc.tensor), ("act", nc.scalar), ("dve", nc.vector),
                          ("pool", nc.gpsimd), ("sp", nc.sync)):
            sem = st.enter_context(nc.semaphore("s_" + name))
            self.sems.append(sem)
            self.E[name] = dict(eng=eng, sid=len(self.sems) - 1, cnt=0, waited={})
        self.dsid = []
        for i in range(self.NDS):
            sem = st.enter_context(nc.semaphore("d%d" % i))
            self.sems.append(sem)
            self.dsid.append(len(self.sems) - 1)
        self.dval = [0] * self.NDS
        self.dnext = 0
        self.nbuf = 0

    def sb(self, name, shape, dt):
        return self.st.enter_context(self.nc.sbuf_tensor(name, shape, dt))

    def ps(self, name, shape=(128, 512), dt=F32):
        return self.st.enter_context(self.nc.psum_tensor(name, list(shape), dt))

    def wait(self, en, ev, is_dma=False):
        if ev is None:
            return
        sid, v = ev
        e = self.E[en]
        if sid == e["sid"] and en == "pe":
            return
        if e["waited"].get(sid, 0) >= v:
            return
        e["eng"].wait_ge(self.sems[sid], v)
        e["waited"][sid] = v

    def _deps(self, en, reads, writes, is_dma=False):
        for b in reads:
            self.wait(en, b.w, is_dma)
        for b in writes:
            self.wait(en, b.w, is_dma)
            for sid, v in b.r.items():
                self.wait(en, (sid, v), is_dma)

    def _mark(self, ev, reads, writes):
        sid, v = ev
        for b in reads:
            if b.r.get(sid, 0) < v:
                b.r[sid] = v
        for b in writes:
            b.w = ev
            b.r = {}

    def op(self, en, fn, reads=(), writes=()):
        self._deps(en, reads, writes)
        e = self.E[en]
        ins = fn(e["eng"])
        e["cnt"] += 1
        ins.then_inc(self.sems[e["sid"]], 1)
        self._mark((e["sid"], e["cnt"]), reads, writes)

    def mm(self, fns, reads=(), writes=()):
        self._deps("pe", reads, writes)
        e = self.E["pe"]
        ins = None
        for fn in fns:
            ins = fn(e["eng"])
        e["cnt"] += 1
        ins.then_inc(self.sems[e["sid"]], 1)
        self._mark((e["sid"], e["cnt"]), reads, writes)

    def dma(self, qn, out, in_, reads=(), writes=(), track=None):
        self._deps(qn, reads, writes, is_dma=True)
        i = self.dnext
        self.dnext = (i + 1) % self.NDS
        if self.dval[i] > 0:
            self.wait(qn, (self.dsid[i], self.dval[i]), True)
        self.dval[i] += 16
        self.E[qn]["eng"].dma_start(out=out, in_=in_).then_inc(self.sems[self.dsid[i]], 16)
        self._mark((self.dsid[i], self.dval[i]), reads, writes)
        if track is not None:
            track[self.dsid[i]] = self.dval[i]

    def finish(self, track):
        for sid, v in track.items():
            self.wait("sp", (sid, v), True)


class Ring:
    def __init__(self, kb, name, n, shape, dt, psum=False):
        self.t = []
        for i in range(n):
            t = kb.ps(name + str(i), shape, dt) if psum else kb.sb(name + str(i), list(shape), dt)
            self.t.append((t, Buf()))
        self.i = 0

    def next(self):
        r = self.t[self.i]
        self.i = (self.i + 1) % len(self.t)
        return r


def w_src(w, c0, ncols, kc=None):
    return w[:, c0:c0 + ncols].rearrange("(kc p) f -> p kc f", p=128)


def make_perm(kb, perm, n):
    h = n // 2
    b = Buf()
    kb.op("pool", lambda g: g.memset(perm[:], 0.0), writes=[b])
    kb.op("pool", lambda g: g.affine_select(out=perm[0:n, 0:h], in_=perm[0:n, 0:h], pattern=[[-1, h]],
                                             compare_op=ALU.not_equal, fill=1.0, base=-h,
                                             channel_multiplier=1), writes=[b])
    kb.op("pool", lambda g: g.affine_select(out=perm[0:n, h:n], in_=perm[0:n, h:n], pattern=[[-1, h]],
                                             compare_op=ALU.not_equal, fill=1.0, base=0,
                                             channel_multiplier=1), writes=[b])
    return b


class Rope:
    def __init__(self, kb, n, tok0_ap):
        self.kb, self.n = kb, n
        h = n // 2
        self.h = h
        I32 = mybir.dt.int32
        self.jidx = kb.sb("rp_j%d" % n, [128, T], F32)
        self.pidx = kb.sb("rp_p%d" % n, [128, 1], F32)
        self.pm = kb.sb("rp_pm%d" % n, [128, 1], F32)
        self.invf = kb.sb("rp_f%d" % n, [128, 1], F32)
        self.tokb = kb.sb("rp_tb%d" % n, [128, 1], F32)
        self.tok0 = kb.sb("rp_t0%d" % n, [128, 1], F32)
        self.r = kb.sb("rp_r%d" % n, [128, T], F32)
        self.ni = kb.sb("rp_ni%d" % n, [128, T], I32)
        self.nf = kb.sb("rp_nf%d" % n, [128, T], F32)
        self.f = kb.sb("rp_fr%d" % n, [128, T], F32)
        self.C = kb.sb("rp_c%d" % n, [128, T], F32)
        self.Sg = kb.sb("rp_s%d" % n, [128, T], F32)
        self.bconst = Buf()
        self.btab = Buf()
        self.btmp = Buf()
        kb.dma("sp", self.tok0[:], tok0_ap, writes=[self.bconst])
        kb.op("pool", lambda g: g.iota(self.jidx[:], [[1, T]], base=0, channel_multiplier=0,
                                       allow_small_or_imprecise_dtypes=True), writes=[self.bconst])
        kb.op("pool", lambda g: g.iota(self.pidx[:], [[0, 1]], base=0, channel_multiplier=1,
                                       allow_small_or_imprecise_dtypes=True), writes=[self.bconst])
        kb.op("dve", lambda v: v.tensor_single_scalar(out=self.pm[:], in_=self.pidx[:], scalar=float(h),
                                                      op=ALU.is_ge), reads=[self.bconst], writes=[self.btmp])
        kb.op("dve", lambda v: v.scalar_tensor_tensor(out=self.pidx[:], in0=self.pm[:], scalar=-float(h),
                                                      in1=self.pidx[:], op0=ALU.mult, op1=ALU.add),
              writes=[self.bconst])
        kb.op("act", lambda a: a.activation(out=self.invf[:], in_=self.pidx[:], func=AF.Exp,
                                            scale=-math.log(THETA) / h), reads=[self.bconst], writes=[self.btmp])
        kb.op("dve", lambda v: v.tensor_scalar_mul(out=self.invf[:], in0=self.invf[:], scalar1=1.0 / (2 * PI)),
              reads=[self.btmp], writes=[self.bconst])

    def _frac(self):
        kb = self.kb
        kb.op("dve", lambda v: v.tensor_copy(out=self.ni[:], in_=self.r[:]), writes=[self.btmp])
        kb.op("dve", lambda v: v.tensor_copy(out=self.nf[:], in_=self.ni[:]), writes=[self.btmp])
        kb.op("dve", lambda v: v.tensor_tensor(out=self.f[:], in0=self.r[:], in1=self.nf[:], op=ALU.subtract),
              writes=[self.btmp])

    def tile(self, t0):
        kb, n, h = self.kb, self.n, self.h
        kb.op("dve", lambda v: v.tensor_scalar_add(out=self.tokb[:], in0=self.tok0[:], scalar1=float(t0)),
              reads=[self.bconst], writes=[self.btmp])
        kb.op("dve", lambda v: v.tensor_scalar(out=self.r[:], in0=self.jidx[:], scalar1=self.tokb[:, 0:1],
                                               scalar2=self.invf[:, 0:1], op0=ALU.add, op1=ALU.mult),
              reads=[self.bconst], writes=[self.btmp])
        self._frac()
        kb.op("act", lambda a: a.activation(out=self.Sg[0:h, :], in_=self.f[0:h, :], func=AF.Sin, scale=-2 * PI),
              reads=[self.btmp], writes=[self.btab])
        kb.op("act", lambda a: a.activation(out=self.Sg[h:n, :], in_=self.f[h:n, :], func=AF.Sin, scale=2 * PI),
              reads=[self.btmp], writes=[self.btab])
        kb.op("dve", lambda v: v.tensor_scalar_add(out=self.r[:], in0=self.r[:], scalar1=0.25), writes=[self.btmp])
        self._frac()
        kb.op("act", lambda a: a.activation(out=self.C[0:n, :], in_=self.f[0:n, :], func=AF.Sin, scale=2 * PI),
              reads=[self.btmp], writes=[self.btab])


def build_l1(ntiles=TOK // T, blocks_sel=None):
    nc = bass.Bass("TRN2", target_bir_lowering=False)
    xT = nc.dram_tensor("xT", [D, TOK], F32, kind="ExternalInput").ap()
    w = nc.dram_tensor("w_in", [D, 3 * A_WIDTH + 3 * A_WIDTH], F32, kind="ExternalInput").ap()
    tok0 = nc.dram_tensor("tok0", [128, 1], F32, kind="ExternalInput").ap()
    qT = nc.dram_tensor("qT", [A_WIDTH, TOK], BF16, kind="ExternalOutput").ap()
    kT = nc.dram_tensor("kT", [A_WIDTH, TOK], BF16, kind="ExternalOutput").ap()
    vT = nc.dram_tensor("vT", [A_WIDTH, TOK], BF16, kind="ExternalOutput").ap()
    gbT = nc.dram_tensor("gbT", [A_WIDTH, TOK], F32, kind="ExternalOutput").ap()
    uT = nc.dram_tensor("uT", [A_WIDTH, TOK], F32, kind="ExternalOutput").ap()
    with ExitStack() as st:
        kb = KB(nc, st)
        perm = kb.sb("perm", [128, 128], BF16)
        bperm = make_perm(kb, perm, 128)
        rope = Rope(kb, 128, tok0)
        xring = Ring(kb, "xb", 2, (128, KC, T), BF16)
        wring = Ring(kb, "wb", 2, (128, KC, 512), BF16)
        psr = Ring(kb, "ps", 6, (128, 512), F32, psum=True)
        pswr = Ring(kb, "psw", 2, (128, 512), F32, psum=True)
        qbr = Ring(kb, "qb", 2, (128, T), BF16)
        t1r = Ring(kb, "t1", 2, (128, T), F32)
        t2r = Ring(kb, "t2", 2, (128, T), F32)
        obr = Ring(kb, "ob", 4, (128, T), BF16)
        ofr = Ring(kb, "of", 4, (128, T), F32)
        gcb = [(kb.sb("gc%d" % i, [128, T], F32), Buf()) for i in range(4)]
        bout = {}
        blocks = []
        for typ, base in (("q", 0), ("k", 2048), ("v", 4096), ("gb", 6144)):
            for i in range(4):
                blocks.append((typ, base + 512 * i, i))
        for i in range(4):
            blocks.append(("gc", 8192 + 512 * i, i))
            blocks.append(("hin", 10240 + 512 * i, i))
        outs = {"q": qT, "k": kT, "v": vT, "gb": gbT, "hin": uT}

        def load_x(ti):
            xt, xbuf = xring.next()
            kb.dma("pool", xt[:], xT[:, ti * T:(ti + 1) * T].rearrange("(kc p) t -> p kc t", p=128),
                   writes=[xbuf])
            return xt, xbuf

        nxt = load_x(0)
        if blocks_sel is not None:
            blocks = [blocks[i] for i in blocks_sel]
        for ti in range(ntiles):
            xt, xbuf = nxt
            rope.tile(ti * T)
            for bi, (typ, c0, i) in enumerate(blocks):
                wt, wbuf = wring.next()
                kb.dma("pool", wt[:], w_src(w, c0, 512), writes=[wbuf])
                if bi == min(4, len(blocks) - 1) and ti + 1 < ntiles:
                    nxt = load_x(ti + 1)
                for half in range(2):
                    pss = [psr.next() for _ in range(2)]
                    fns = []
                    for k in range(KC):
                        for j in range(2):
                            cc = (half * 2 + j) * 128
                            fns.append(lambda pe, k=k, j=j, cc=cc: pe.matmul(
                                pss[j][0][:], wt[:, k, cc:cc + 128], xt[:, k, :],
                                start=(k == 0), stop=(k == KC - 1)))
                    kb.mm(fns, reads=[wbuf, xbuf], writes=[pss[0][1], pss[1][1]])
                    for j in range(2):
                        pt, pbuf = pss[j]
                        ch = half * 2 + j
                        row0 = (c0 % 2048) + ch * 128
                        cols = slice(ti * T, (ti + 1) * T)
                        if typ in ("q", "k"):
                            qb, qbuf = qbr.next()
                            kb.op("act", lambda a: a.copy(out=qb[:], in_=pt[:]), reads=[pbuf], writes=[qbuf])
                            pw, pwbuf = pswr.next()
                            kb.mm([lambda pe: pe.matmul(pw[:], perm[:], qb[:], start=True, stop=True)],
                                  reads=[bperm, qbuf], writes=[pwbuf])
                            t1, t1b = t1r.next()
                            t2, t2b = t2r.next()
                            ob, obb = obr.next()
                            kb.op("dve", lambda v: v.tensor_tensor(out=t1[:], in0=qb[:], in1=rope.C[:], op=ALU.mult),
                                  reads=[qbuf, rope.btab], writes=[t1b])
                            kb.op("dve", lambda v: v.tensor_tensor(out=t2[:], in0=pw[:], in1=rope.Sg[:], op=ALU.mult),
                                  reads=[pwbuf, rope.btab], writes=[t2b])
                            kb.op("pool", lambda g: g.tensor_tensor(out=ob[:], in0=t1[:], in1=t2[:], op=ALU.add),
                                  reads=[t1b, t2b], writes=[obb])
                            kb.dma("sp", outs[typ][row0:row0 + 128, cols], ob[:], reads=[obb], track=bout)
                        elif typ == "v":
                            ob, obb = obr.next()
                            kb.op("act", lambda a: a.copy(out=ob[:], in_=pt[:]), reads=[pbuf], writes=[obb])
                            kb.dma("sp", vT[row0:row0 + 128, cols], ob[:], reads=[obb], track=bout)
                        elif typ == "gb":
                            of, ofb = ofr.next()
                            kb.op("act", lambda a: a.copy(out=of[:], in_=pt[:]), reads=[pbuf], writes=[ofb])
                            kb.dma("sp", gbT[row0:row0 + 128, cols], of[:], reads=[ofb], track=bout)
                        elif typ == "gc":
                            gt, gbuf = gcb[ch]
                            kb.op("act", lambda a: a.copy(out=gt[:], in_=pt[:]), reads=[pbuf], writes=[gbuf])
                        else:
                            gt, gbuf = gcb[ch]
                            of, ofb = ofr.next()
                            kb.op("dve", lambda v: v.tensor_tensor(out=of[:], in0=gt[:], in1=pt[:], op=ALU.mult),
                                  reads=[pbuf, gbuf], writes=[ofb])
                            kb.dma("sp", uT[row0:row0 + 128, cols], of[:], reads=[ofb], track=bout)
        kb.finish(bout)
    return nc


def run_l1(xTfull, w_in_a):
    nc = build_l1()
    in_maps = []
    for c in range(NTC):
        in_maps.append({"xT": np.ascontiguousarray(xTfull[:, c * TOK:(c + 1) * TOK]),
                        "w_in": w_in_a, "tok0": np.full((128, 1), c * TOK, np.float32)})
    res = run_bass_kernel_spmd(nc, in_maps, core_ids=list(range(NTC)))
    out = {}
    for name in ("qT", "kT", "vT", "gbT", "uT"):
        out[name] = np.concatenate([r[name] for r in res.results], axis=1)
    return out


def barrier(kb):
    evs = [(e["sid"], e["cnt"]) for e in kb.E.values() if e["cnt"] > 0]
    evs += [(kb.dsid[i], kb.dval[i]) for i in range(kb.NDS) if kb.dval[i] > 0]
    for en in kb.E:
        for ev in evs:
            kb.wait(en, ev, is_dma=True)


def linear(kb, xt, xbuf, kcn, w, groups, wring, psr, evac, wq="pool"):
    for grp in groups:
        g0 = grp[0][0]
        gw = grp[-1][0] + grp[-1][1] - g0
        wt, wbuf = wring.next()
        kb.dma(wq, wt[:, 0:kcn, 0:gw], w[:, g0:g0 + gw].rearrange("(kc p) f -> p kc f", p=128), writes=[wbuf])
        pss = [psr.next() for _ in grp]
        fns = []
        for k in range(kcn):
            for j, (c0, m) in enumerate(grp):
                fns.append(lambda pe, k=k, j=j, c0=c0, m=m: pe.matmul(
                    pss[j][0][0:m, :], wt[:, k, c0 - g0:c0 - g0 + m], xt[:, k, :],
                    start=(k == 0), stop=(k == kcn - 1)))
        kb.mm(fns, reads=[wbuf, xbuf], writes=[p[1] for p in pss])
        for j, (c0, m) in enumerate(grp):
            evac(c0, m, pss[j][0], pss[j][1])


class Norm:
    def __init__(self, kb, psr, tag):
        self.kb, self.psr = kb, psr
        self.ones = kb.sb("n1_" + tag, [128, 128], F32)
        self.bones = Buf()
        kb.op("pool", lambda g: g.memset(self.ones[:], 1.0), writes=[self.bones])
        self.sq = Ring(kb, "nsq_" + tag, 2, (128, T), F32)
        self.mean = kb.sb("nmu_" + tag, [128, T], F32)
        self.rstd = kb.sb("nrs_" + tag, [128, T], F32)
        self.tmp = Ring(kb, "ntp_" + tag, 2, (128, T), F32)
        self.bstat = Buf()

    def stats(self, yT, ybufs, nch, nfeat, eps, center):
        kb = self.kb
        p2, p2b = self.psr.next()
        if center:
            p1, p1b = self.psr.next()
            kb.mm([lambda pe, c=c: pe.matmul(p1[:], self.ones[:], yT[:, c, :], start=(c == 0), stop=(c == nch - 1))
                   for c in range(nch)], reads=[self.bones] + ybufs, writes=[p1b])
        for c in range(nch):
            sq, sqb = self.sq.next()
            kb.op("act", lambda a: a.activation(out=sq[:], in_=yT[:, c, :], func=AF.Square),
                  reads=[ybufs[c]], writes=[sqb])
            kb.mm([lambda pe: pe.matmul(p2[:], self.ones[:], sq[:], start=(c == 0), stop=(c == nch - 1))],
                  reads=[self.bones, sqb], writes=[p2b])
        inv = 1.0 / nfeat
        if center:
            kb.op("dve", lambda v: v.tensor_scalar_mul(out=self.mean[:], in0=p1[:], scalar1=inv),
                  reads=[p1b], writes=[self.bstat])
            tm, tmb = self.tmp.next()
            kb.op("dve", lambda v: v.tensor_tensor(out=tm[:], in0=self.mean[:], in1=self.mean[:], op=ALU.mult),
                  reads=[self.bstat], writes=[tmb])
            kb.op("dve", lambda v: v.scalar_tensor_tensor(out=self.rstd[:], in0=p2[:], scalar=inv, in1=tm[:],
                                                          op0=ALU.mult, op1=ALU.subtract),
                  reads=[p2b, tmb], writes=[self.bstat])
            kb.op("dve", lambda v: v.tensor_scalar_add(out=self.rstd[:], in0=self.rstd[:], scalar1=eps),
                  writes=[self.bstat])
        else:
            kb.op("dve", lambda v: v.tensor_scalar(out=self.rstd[:], in0=p2[:], scalar1=inv, scalar2=eps,
                                                   op0=ALU.mult, op1=ALU.add), reads=[p2b], writes=[self.bstat])
        kb.op("act", lambda a: a.activation(out=self.rstd[:], in_=self.rstd[:], func=AF.Sqrt),
              reads=[self.bstat], writes=[self.bstat])
        kb.op("dve", lambda v: v.reciprocal(out=self.rstd[:], in_=self.rstd[:]), reads=[self.bstat],
              writes=[self.bstat])

    def apply(self, src, srcbuf, c, gam, bet, center, outs):
        kb = self.kb
        tm, tmb = self.tmp.next()
        if center:
            kb.op("dve", lambda v: v.tensor_tensor(out=tm[:], in0=src, in1=self.mean[:], op=ALU.subtract),
                  reads=[srcbuf, self.bstat], writes=[tmb])
            kb.op("dve", lambda v: v.tensor_tensor(out=tm[:], in0=tm[:], in1=self.rstd[:], op=ALU.mult),
                  reads=[self.bstat], writes=[tmb])
        else:
            kb.op("dve", lambda v: v.tensor_tensor(out=tm[:], in0=src, in1=self.rstd[:], op=ALU.mult),
                  reads=[srcbuf, self.bstat], writes=[tmb])
        for ap, buf in outs:
            if bet is not None:
                kb.op("act", lambda a: a.activation(out=ap, in_=tm[:], func=AF.Identity, scale=gam[:, c:c + 1],
                                                    bias=bet[:, c:c + 1]), reads=[tmb], writes=[buf])
            else:
                kb.op("act", lambda a: a.activation(out=ap, in_=tm[:], func=AF.Identity, scale=gam[:, c:c + 1]),
                      reads=[tmb], writes=[buf])


def load_cols(kb, name, ap, n):
    t = kb.sb(name, [128, n], F32)
    b = Buf()
    kb.dma("sp", t[:], ap, writes=[b])
    return t, b


def build_chain(first, ntiles=TOK // T, do_ffn=True, do_mla=True):
    nc = bass.Bass("TRN2", target_bir_lowering=False)
    dt_in = nc.dram_tensor
    inT = dt_in("inT", [D if not first else A_WIDTH, TOK], BF16, kind="ExternalInput").ap()
    xT = dt_in("xT", [D, TOK], F32, kind="ExternalInput").ap()
    w_out = dt_in("w_out", [D, D], F32, kind="ExternalInput").ap()
    w_gate = dt_in("w_gate", [D, FFN], F32, kind="ExternalInput").ap()
    w_up = dt_in("w_up", [D, FFN], F32, kind="ExternalInput").ap()
    w_down = dt_in("w_down", [FFN, D], F32, kind="ExternalInput").ap()
    lnp = dt_in("lnp", [128, 4 * KC], F32, kind="ExternalInput").ap()
    if first:
        uTh = dt_in("uTh", [A_WIDTH, TOK + 2], F32, kind="ExternalInput").ap()
        gbT = dt_in("gbT", [A_WIDTH, TOK], F32, kind="ExternalInput").ap()
        cwp = dt_in("cwp", [128, 16 * 3], F32, kind="ExternalInput").ap()
        tok0 = dt_in("tok0", [128, 1], F32, kind="ExternalInput").ap()
        w_inc = dt_in("w_inc", [D, MLA_IN], F32, kind="ExternalInput").ap()
        w_uq = dt_in("w_uq", [Q_LORA, MLA_H * 192], F32, kind="ExternalInput").ap()
        w_ukv = dt_in("w_ukv", [KV_LORA, MLA_H * 256], F32, kind="ExternalInput").ap()
        nrm = dt_in("nrm", [128, 16], F32, kind="ExternalInput").ap()
        QnT = dt_in("QnT", [MLA_H * 128, TOK], BF16, kind="ExternalOutput").ap()
        QrT = dt_in("QrT", [MLA_H * 64, TOK], BF16, kind="ExternalOutput").ap()
        KnT = dt_in("KnT", [MLA_H * 128, TOK], BF16, kind="ExternalOutput").ap()
        VT = dt_in("VT", [MLA_H * 128, TOK], BF16, kind="ExternalOutput").ap()
        KrT = dt_in("KrT", [64, TOK], BF16, kind="ExternalOutput").ap()
    x2T = dt_in("x2T", [D, TOK], F32, kind="ExternalOutput").ap()
    X1 = dt_in("X1", [D, TOK], F32, kind="Internal").ap()
    X1B = dt_in("X1B", [D, TOK], BF16, kind="Internal").ap()
    H = dt_in("H", [FFN, TOK], BF16, kind="Internal").ap()
    track = {}
    bX1 = [Buf() for _ in range(ntiles)]
    bX1B = [Buf() for _ in range(ntiles)]
    bH = [[Buf() for _ in range(FC)] for _ in range(ntiles)]
    bX2 = [Buf() for _ in range(ntiles)]

    def cs(ti):
        return slice(ti * T, (ti + 1) * T)

    def cm(ap):
        return ap.rearrange("(c p) t -> p c t", p=128)

    with ExitStack() as st0:
        kb = KB(nc, st0)
        psr = Ring(kb, "ps", 8, (128, 512), F32, psum=True)
        lnt, _ = load_cols(kb, "lnp_sb", lnp, 4 * KC)
        if first:
            cwt, _ = load_cols(kb, "cwp_sb", cwp, 48)
            nrt, _ = load_cols(kb, "nrm_sb", nrm, 16)
        barrier(kb)

        with ExitStack() as st:
            kb.st = st
            norm = Norm(kb, psr, "a")
            inb = Ring(kb, "inb", 1, (128, KC, T), BF16)
            wring = Ring(kb, "w1", 2, (128, KC, 256), BF16)
            yT = kb.sb("yT", [128, KC, T], F32)
            ybufs = [Buf() for _ in range(KC)]
            xres = Ring(kb, "xres", 2, (128, 2, T), F32)
            if first:
                ur = Ring(kb, "ur", 2, (128, T + 2), F32)
                gr = Ring(kb, "gr", 2, (128, T), F32)
                ct = Ring(kb, "ct", 2, (128, T), F32)
            for ti in range(ntiles):
                it, ibuf = inb.next()
                if first:
                    kb.dma("sp", it[:, 0:16, :], cm(inT[:, cs(ti)]), writes=[ibuf])
                    for c in range(16):
                        ut, ub = ur.next()
                        gt, gb_ = gr.next()
                        tt, tb = ct.next()
                        kb.dma("sp", ut[:], uTh[c * 128:(c + 1) * 128, ti * T:ti * T + T + 2], writes=[ub])
                        kb.dma("sp", gt[:], gbT[c * 128:(c + 1) * 128, cs(ti)], writes=[gb_])
                        kb.op("dve", lambda v: v.tensor_scalar_mul(out=tt[:], in0=ut[:, 2:T + 2],
                                                                   scalar1=cwt[:, 3 * c + 2:3 * c + 3]),
                              reads=[ub], writes=[tb])
                        kb.op("dve", lambda v: v.scalar_tensor_tensor(out=tt[:], in0=ut[:, 1:T + 1],
                                                                      scalar=cwt[:, 3 * c + 1:3 * c + 2], in1=tt[:],
                                                                      op0=ALU.mult, op1=ALU.add),
                              reads=[ub], writes=[tb])
                        kb.op("dve", lambda v: v.scalar_tensor_tensor(out=tt[:], in0=ut[:, 0:T],
                                                                      scalar=cwt[:, 3 * c:3 * c + 1], in1=tt[:],
                                                                      op0=ALU.mult, op1=ALU.add),
                              reads=[ub], writes=[tb])
                        kb.op("dve", lambda v: v.tensor_tensor(out=it[:, 16 + c, :], in0=tt[:], in1=gt[:], op=ALU.mult),
                              reads=[tb, gb_], writes=[ibuf])
                else:
                    kb.dma("sp", it[:], cm(inT[:, cs(ti)]), writes=[ibuf])

                def evac1(c0, m, pt, pbuf):
                    ch = c0 // 128
                    if ch % 2 == 0:
                        evac1.x = xres.next()
                        kb.dma("sp", evac1.x[0][:], cm(xT[ch * 128:(ch + 2) * 128, cs(ti)]), writes=[evac1.x[1]])
                    xt_, xb_ = evac1.x
                    kb.op("dve", lambda v: v.scalar_tensor_tensor(out=yT[:, ch, :], in0=xt_[:, ch % 2, :], scalar=ALPHA,
                                                                  in1=pt[:], op0=ALU.mult, op1=ALU.add),
                          reads=[pbuf, xb_], writes=[ybufs[ch]])
                groups = [[(g * 256, 128), (g * 256 + 128, 128)] for g in range(16)]
                linear(kb, it, ibuf, KC, w_out, groups, wring, psr, evac1)
                norm.stats(yT, ybufs, KC, D, LN_EPS, True)
                for c in range(KC):
                    norm.apply(yT[:, c, :], ybufs[c], c, lnt[:, 0:KC], lnt[:, KC:2 * KC], True,
                               [(yT[:, c, :], ybufs[c])])
                    kb.op("act", lambda a: a.copy(out=it[:, c, :], in_=yT[:, c, :]), reads=[ybufs[c]], writes=[ibuf])
                kb.dma("sp", cm(X1[:, cs(ti)]), yT[:], reads=ybufs, writes=[bX1[ti]])
                kb.dma("sp", cm(X1B[:, cs(ti)]), it[:], reads=[ibuf], writes=[bX1B[ti]])
            barrier(kb)

        if do_ffn:
            with ExitStack() as st:
                kb.st = st
                xbr = Ring(kb, "x1b", 2, (128, KC, T), BF16)
                wg = Ring(kb, "wg", 2, (128, KC, 256), BF16)
                wu = Ring(kb, "wu", 2, (128, KC, 256), BF16)
                sgr = Ring(kb, "sg", 2, (128, T), F32)
                hbr = Ring(kb, "hb", 4, (128, T), BF16)
                for ti in range(ntiles):
                    xt, xbuf = xbr.next()
                    kb.dma("sp", xt[:], cm(X1B[:, cs(ti)]), reads=[bX1B[ti]], writes=[xbuf])
                    for g in range(FC // 2):
                        wgt, wgb = wg.next()
                        wut, wub = wu.next()
                        kb.dma("pool", wgt[:], w_src(w_gate, g * 256, 256), writes=[wgb])
                        kb.dma("pool", wut[:], w_src(w_up, g * 256, 256), writes=[wub])
                        pg = [psr.next() for _ in range(2)]
                        pu = [psr.next() for _ in range(2)]
                        fns = []
                        for k in range(KC):
                            for j in range(2):
                                fns.append(lambda pe, k=k, j=j: pe.matmul(pg[j][0][:], wgt[:, k, j * 128:(j + 1) * 128],
                                                                          xt[:, k, :], start=(k == 0), stop=(k == KC - 1)))
                                fns.append(lambda pe, k=k, j=j: pe.matmul(pu[j][0][:], wut[:, k, j * 128:(j + 1) * 128],
                                                                          xt[:, k, :], start=(k == 0), stop=(k == KC - 1)))
                        kb.mm(fns, reads=[wgb, wub, xbuf], writes=[p[1] for p in pg + pu])
                        for j in range(2):
                            fch = g * 2 + j
                            sg, sgb = sgr.next()
                            hb, hbb = hbr.next()
                            kb.op("act", lambda a: a.activation(out=sg[:], in_=pg[j][0][:], func=AF.Silu),
                                  reads=[pg[j][1]], writes=[sgb])
                            kb.op("dve", lambda v: v.tensor_tensor(out=hb[:], in0=sg[:], in1=pu[j][0][:], op=ALU.mult),
                                  reads=[sgb, pu[j][1]], writes=[hbb])
                            kb.dma("sp", H[fch * 128:(fch + 1) * 128, cs(ti)], hb[:], reads=[hbb], writes=[bH[ti][fch]])
                barrier(kb)

            with ExitStack() as st:
                kb.st = st
                norm = Norm(kb, psr, "b")
                hT = kb.sb("hT", [128, FC, T], BF16)
                hbuf = Buf()
                wd = Ring(kb, "wd", 2, (128, 8, 512), BF16)
                yT = kb.sb("yT2", [128, KC, T], F32)
                ybufs = [Buf() for _ in range(KC)]
                xres = Ring(kb, "xres2", 2, (128, 4, T), F32)
                fgs = [(f0, min(8, FC - f0)) for f0 in range(0, FC, 8)]
                for ti in range(ntiles):
                    kb.dma("sp", hT[:], cm(H[:, cs(ti)]), reads=bH[ti], writes=[hbuf])
                    for dg in range(8):
                        pss = [psr.next() for _ in range(4)]
                        xt_, xb_ = xres.next()
                        kb.dma("sp", xt_[:], cm(X1[dg * 512:(dg + 1) * 512, cs(ti)]), reads=[bX1[ti]], writes=[xb_])
                        for gi, (f0, nf) in enumerate(fgs):
                            wt, wb = wd.next()
                            kb.dma("pool", wt[:, 0:nf, :],
                                   w_down[f0 * 128:(f0 + nf) * 128, dg * 512:(dg + 1) * 512].rearrange(
                                       "(fc p) d -> p fc d", p=128), writes=[wb])
                            fns = []
                            for fi in range(nf):
                                for j in range(4):
                                    f = f0 + fi
                                    fns.append(lambda pe, fi=fi, j=j, f=f: pe.matmul(
                                        pss[j][0][:], wt[:, fi, j * 128:(j + 1) * 128], hT[:, f, :],
                                        start=(f == 0), stop=(f == FC - 1)))
                            kb.mm(fns, reads=[wb, hbuf], writes=[p[1] for p in pss])
                        for j in range(4):
                            ch = dg * 4 + j
                            kb.op("dve", lambda v: v.scalar_tensor_tensor(out=yT[:, ch, :], in0=xt_[:, j, :], scalar=ALPHA,
                                                                          in1=pss[j][0][:], op0=ALU.mult, op1=ALU.add),
                                  reads=[pss[j][1], xb_], writes=[ybufs[ch]])
                    norm.stats(yT, ybufs, KC, D, LN_EPS, True)
                    for c in range(KC):
                        norm.apply(yT[:, c, :], ybufs[c], c, lnt[:, 2 * KC:3 * KC], lnt[:, 3 * KC:4 * KC], True,
                                   [(yT[:, c, :], ybufs[c])])
                    kb.dma("sp", cm(x2T[:, cs(ti)]), yT[:], reads=ybufs, writes=[bX2[ti]], track=track)
                barrier(kb)
        if first and do_mla:
            build_mla_pre(kb, nc, psr, ntiles, x2T if do_ffn else X1, bX2 if do_ffn else bX1, w_inc, w_uq, w_ukv, nrt, tok0,
                          QnT, QrT, KnT, VT, KrT, track)
        kb.st = st0
        kb.finish(track)
    return nc


def build_mla_pre(kb, nc, psr, ntiles, xsrc, bxs, w_inc, w_uq, w_ukv, nrt, tok0, QnT, QrT, KnT, VT, KrT, track):
    def cs(ti):
        return slice(ti * T, (ti + 1) * T)

    with ExitStack() as st:
        kb.st = st
        norm = Norm(kb, psr, "m")
        perm = kb.sb("perm64", [128, 64], BF16)
        bperm = make_perm(kb, perm, 64)
        rope = Rope(kb, 64, tok0)
        xbr = Ring(kb, "x2b", 1, (128, KC, T), BF16)
        wr = Ring(kb, "wm", 2, (128, KC, 256), BF16)
        wq = Ring(kb, "wq", 2, (128, 12, 384), BF16)
        wk = Ring(kb, "wk", 2, (128, 4, 512), BF16)
        cq = kb.sb("cq", [128, 17, T], F32)
        cbufs = [Buf() for _ in range(17)]
        cqn = kb.sb("cqn", [128, 16, T], BF16)
        nbuf = Buf()
        obr = Ring(kb, "mo", 4, (128, T), BF16)
        qbr = Ring(kb, "mq", 2, (64, T), BF16)
        t1r = Ring(kb, "mt1", 2, (64, T), F32)
        t2r = Ring(kb, "mt2", 2, (64, T), F32)

        def rope64(src_ps, src_buf, dst_ap):
            qb, qbuf = qbr.next()
            kb.op("act", lambda a: a.copy(out=qb[:], in_=src_ps), reads=[src_buf], writes=[qbuf])
            pw, pwb = psr.next()
            kb.mm([lambda pe: pe.matmul(pw[0:64, :], perm[0:64, :], qb[:], start=True, stop=True)],
                  reads=[bperm, qbuf], writes=[pwb])
            t1, t1b = t1r.next()
            t2, t2b = t2r.next()
            ob, obb = obr.next()
            kb.op("dve", lambda v: v.tensor_tensor(out=t1[:], in0=qb[:], in1=rope.C[0:64, :], op=ALU.mult),
                  reads=[qbuf, rope.btab], writes=[t1b])
            kb.op("dve", lambda v: v.tensor_tensor(out=t2[:], in0=pw[0:64, :], in1=rope.Sg[0:64, :], op=ALU.mult),
                  reads=[pwb, rope.btab], writes=[t2b])
            kb.op("pool", lambda g: g.tensor_tensor(out=ob[0:64, :], in0=t1[:], in1=t2[:], op=ALU.add),
                  reads=[t1b, t2b], writes=[obb])
            kb.dma("sp", dst_ap, ob[0:64, :], reads=[obb], track=track)

        for ti in range(ntiles):
            xt, xbuf = xbr.next()
            kb.dma("pool", xt[:], xsrc[:, cs(ti)].rearrange("(c p) t -> p c t", p=128), reads=[bxs[ti]], writes=[xbuf])
            rope.tile(ti * T)

            def evac_in(c0, m, pt, pbuf):
                ch = c0 // 128
                kb.op("act", lambda a: a.copy(out=cq[0:m, ch, :], in_=pt[0:m, :]), reads=[pbuf], writes=[cbufs[ch]])
            groups = [[(g * 256, 128), (g * 256 + 128, 128)] for g in range(8)] + [[(2048, 64)]]
            linear(kb, xt, xbuf, KC, w_inc, groups, wr, psr, evac_in)
            for lo, n, nf in ((0, 12, Q_LORA), (12, 4, KV_LORA)):
                norm.stats(cq[:, lo:lo + n, :], cbufs[lo:lo + n], n, nf, RMS_EPS, False)
                for c in range(n):
                    norm.apply(cq[:, lo + c, :], cbufs[lo + c], lo + c, nrt, None, False, [(cqn[:, lo + c, :], nbuf)])
            rope64(cq[0:64, 16, :], cbufs[16], KrT[:, cs(ti)])
            for hp in range(MLA_H // 2):
                wt, wb = wq.next()
                kb.dma("pool", wt[:], w_uq[:, hp * 384:(hp + 1) * 384].rearrange("(kc p) f -> p kc f", p=128),
                       writes=[wb])
                for hh in range(2):
                    h = hp * 2 + hh
                    pn, pnb = psr.next()
                    pr, prb = psr.next()
                    fns = []
                    for k in range(12):
                        fns.append(lambda pe, k=k: pe.matmul(pn[:], wt[:, k, hh * 192:hh * 192 + 128], cqn[:, k, :],
                                                             start=(k == 0), stop=(k == 11)))
                        fns.append(lambda pe, k=k: pe.matmul(pr[0:64, :], wt[:, k, hh * 192 + 128:hh * 192 + 192],
                                                             cqn[:, k, :], start=(k == 0), stop=(k == 11)))
                    kb.mm(fns, reads=[wb, nbuf], writes=[pnb, prb])
                    ob, obb = obr.next()
                    kb.op("act", lambda a: a.copy(out=ob[:], in_=pn[:]), reads=[pnb], writes=[obb])
                    kb.dma("sp", QnT[h * 128:(h + 1) * 128, cs(ti)], ob[:], reads=[obb], track=track)
                    rope64(pr[0:64, :], prb, QrT[h * 64:(h + 1) * 64, cs(ti)])
            for hp in range(MLA_H // 2):
                wt, wb = wk.next()
                kb.dma("pool", wt[:], w_ukv[:, hp * 512:(hp + 1) * 512].rearrange("(kc p) f -> p kc f", p=128),
                       writes=[wb])
                pss = [psr.next() for _ in range(4)]
                fns = []
                for k in range(4):
                    for j in range(4):
                        fns.append(lambda pe, k=k, j=j: pe.matmul(pss[j][0][:], wt[:, k, j * 128:(j + 1) * 128],
                                                                  cqn[:, 12 + k, :], start=(k == 0), stop=(k == 3)))
                kb.mm(fns, reads=[wb, nbuf], writes=[p[1] for p in pss])
                for j in range(4):
                    h = hp * 2 + j // 2
                    dst = (KnT if j % 2 == 0 else VT)[h * 128:(h + 1) * 128, cs(ti)]
                    ob, obb = obr.next()
                    kb.op("act", lambda a: a.copy(out=ob[:], in_=pss[j][0][:]), reads=[pss[j][1]], writes=[obb])
                    kb.dma("sp", dst, ob[:], reads=[obb], track=track)
        barrier(kb)


HPC = MLA_H // NCORES


def build_l4(nq_tiles=S // T, heads=HPC):
    nc = bass.Bass("TRN2", target_bir_lowering=False)
    Qn = nc.dram_tensor("Qn", [HPC * 128, S], BF16, kind="ExternalInput").ap()
    Qr = nc.dram_tensor("Qr", [HPC * 64, S], BF16, kind="ExternalInput").ap()
    Kn = nc.dram_tensor("Kn", [HPC * 128, S], BF16, kind="ExternalInput").ap()
    Kr = nc.dram_tensor("Kr", [64, S], BF16, kind="ExternalInput").ap()
    Vt = nc.dram_tensor("Vt", [HPC * 128, S // 128, 128], BF16, kind="ExternalInput").ap()
    OT = nc.dram_tensor("OT", [HPC * 128, S], BF16, kind="ExternalOutput").ap()
    scale = 192.0 ** -0.5
    track = {}
    with ExitStack() as st:
        kb = KB(nc, st)
        sps = Ring(kb, "sps", 3, (128, 512), F32, psum=True)
        aps = Ring(kb, "aps", 4, (128, 512), F32, psum=True)
        ones = kb.sb("ones", [128, 128], BF16)
        bones = Buf()
        kb.op("pool", lambda g: g.memset(ones[:], 1.0), writes=[bones])
        masks = kb.sb("masks", [128, 4, T], BF16)
        bmask = Buf()
        kb.op("pool", lambda g: g.memset(masks[:], 1.0), writes=[bmask])
        for j in range(4):
            kb.op("pool", lambda g: g.affine_select(out=masks[:, j, :], in_=masks[:, j, :], pattern=[[1, T]],
                                                    compare_op=ALU.is_ge, fill=0.0, base=-128 * j,
                                                    channel_multiplier=-1), writes=[bmask])
        krt = kb.sb("kr", [64, S], BF16)
        bkr = Buf()
        kb.dma("sp", krt[:], Kr, writes=[bkr])
        knt = kb.sb("kn", [128, S], BF16)
        bkn = Buf()
        vt = kb.sb("vt", [128, S // 128, 128], BF16)
        bv = Buf()
        qnr = Ring(kb, "qn", 2, (128, T), BF16)
        qrr = Ring(kb, "qr", 2, (64, T), BF16)
        ptr = Ring(kb, "pt", 4, (128, T), BF16)
        rcr = Ring(kb, "rc", 2, (128, T), F32)
        obr = Ring(kb, "ob", 2, (128, T), BF16)
        for h in range(heads):
            kb.dma("sp", knt[:], Kn[h * 128:(h + 1) * 128, :], writes=[bkn])
            kb.dma("sp", vt[:], Vt[h * 128:(h + 1) * 128, :, :], writes=[bv])
            for i in range(nq_tiles):
                qn, qnb = qnr.next()
                qr, qrb = qrr.next()
                kb.dma("sp", qn[:], Qn[h * 128:(h + 1) * 128, i * T:(i + 1) * T], writes=[qnb])
                kb.dma("sp", qr[:], Qr[h * 64:(h + 1) * 64, i * T:(i + 1) * T], writes=[qrb])
                ao, aob = aps.next()
                ad, adb = aps.next()
                nkb = 4 * i + 4

                def S_(kbi):
                    sp_, spb = sps.next()
                    kb.mm([lambda pe: pe.matmul(sp_[:], knt[:, kbi * 128:(kbi + 1) * 128], qn[:], start=True, stop=False),
                           lambda pe: pe.matmul(sp_[:], krt[:, kbi * 128:(kbi + 1) * 128], qr[:], start=False, stop=True)],
                          reads=[bkn, bkr, qnb, qrb], writes=[spb])
                    return sp_, spb
                cur = S_(0)
                for kbi in range(nkb):
                    nxt = S_(kbi + 1) if kbi + 1 < nkb else None
                    sp_, spb = cur
                    pt, ptb = ptr.next()
                    kb.op("act", lambda a: a.activation(out=pt[:], in_=sp_[:], func=AF.Exp, scale=scale),
                          reads=[spb], writes=[ptb])
                    if kbi >= 4 * i:
                        j = kbi - 4 * i
                        kb.op("dve", lambda v: v.tensor_tensor(out=pt[:], in0=pt[:], in1=masks[:, j, :], op=ALU.mult),
                              reads=[bmask], writes=[ptb])
                    kb.mm([lambda pe: pe.matmul(ao[:], vt[:, kbi, :], pt[:], start=(kbi == 0), stop=(kbi == nkb - 1)),
                           lambda pe: pe.matmul(ad[:], ones[:], pt[:], start=(kbi == 0), stop=(kbi == nkb - 1))],
                          reads=[bv, bones, ptb], writes=[aob, adb])
                    cur = nxt
                rc, rcb = rcr.next()
                ob, obb = obr.next()
                kb.op("dve", lambda v: v.reciprocal(out=rc[:], in_=ad[:]), reads=[adb], writes=[rcb])
                kb.op("dve", lambda v: v.tensor_tensor(out=ob[:], in0=ao[:], in1=rc[:], op=ALU.mult),
                      reads=[aob, rcb], writes=[obb])
                kb.dma("sp", OT[h * 128:(h + 1) * 128, i * T:(i + 1) * T], ob[:], reads=[obb], track=track)
        kb.finish(track)
    return nc


DILS = (1, 4, 16)
SPAN = 2048


def build_l2(nspans=S // SPAN, heads=2):
    nc = bass.Bass("TRN2", target_bir_lowering=False)
    q = nc.dram_tensor("q", [2 * 128, S], BF16, kind="ExternalInput").ap()
    k = nc.dram_tensor("k", [2 * 128, S], BF16, kind="ExternalInput").ap()
    Vd = nc.dram_tensor("Vd", [2 * 3 * 128, S // 128, 128], BF16, kind="ExternalInput").ap()
    aT = nc.dram_tensor("aT", [2 * 128, S], BF16, kind="ExternalOutput").ap()
    scale = 128.0 ** -0.5
    track = {}
    with ExitStack() as st:
        kb = KB(nc, st)
        sps = Ring(kb, "sps", 3, (128, 512), F32, psum=True)
        ops_ = Ring(kb, "ops", 3, (128, 512), F32, psum=True)
        ones = kb.sb("ones", [128, 128], BF16)
        bones = Buf()
        kb.op("pool", lambda g: g.memset(ones[:], 1.0), writes=[bones])
        mask = kb.sb("mask", [128, 256], BF16)
        bmask = Buf()
        kb.op("pool", lambda g: g.memset(mask[:], 1.0), writes=[bmask])
        kb.op("pool", lambda g: g.affine_select(out=mask[:, 0:128], in_=mask[:, 0:128], pattern=[[1, 128]],
                                                compare_op=ALU.is_ge, fill=0.0, base=0, channel_multiplier=-1),
              writes=[bmask])
        kb.op("pool", lambda g: g.affine_select(out=mask[:, 128:256], in_=mask[:, 128:256], pattern=[[-1, 128]],
                                                compare_op=ALU.is_ge, fill=0.0, base=0, channel_multiplier=1),
              writes=[bmask])
        qt = kb.sb("qt", [128, S], BF16)
        kt = kb.sb("kt", [128, S], BF16)
        vts = [kb.sb("v%d" % i, [128, S // 128, 128], BF16) for i in range(3)]
        bq, bk, bvs = Buf(), Buf(), [Buf() for _ in range(3)]
        acc = kb.sb("acc", [128, 2, SPAN], F32)
        bacc = Buf()
        ptr = Ring(kb, "pt", 4, (128, 256), BF16)
        obr = Ring(kb, "ob", 1, (128, SPAN), BF16)
        for h in range(heads):
            kb.dma("sp", qt[:], q[h * 128:(h + 1) * 128, :], writes=[bq])
            kb.dma("sp", kt[:], k[h * 128:(h + 1) * 128, :], writes=[bk])
            for di in range(3):
                r0 = (h * 3 + di) * 128
                kb.dma("sp", vts[di][:], Vd[r0:r0 + 128, :, :], writes=[bvs[di]])
            for sp in range(nspans):
                for di, d in enumerate(DILS):
                    nb = 128 // d
                    for r in range(d):
                        for n in range(SPAN * sp // (128 * d), SPAN * (sp + 1) // (128 * d)):
                            b = r * nb + n
                            t0 = r + 128 * d * n
                            cq = slice(t0, t0 + 127 * d + 1, d)
                            cp = slice(t0 - 128 * d, t0 - d + 1, d)
                            has_prev = n > 0
                            w = 256 if has_prev else 128
                            sp_, spb = sps.next()
                            fns = [lambda pe: pe.matmul(sp_[:, 0:128], kt[:, cq], qt[:, cq], start=True, stop=True)]
                            if has_prev:
                                fns.append(lambda pe: pe.matmul(sp_[:, 128:256], kt[:, cp], qt[:, cq], start=True, stop=True))
                            kb.mm(fns, reads=[bq, bk], writes=[spb])
                            pt, ptb = ptr.next()
                            kb.op("act", lambda a: a.activation(out=pt[:, 0:w], in_=sp_[:, 0:w], func=AF.Exp, scale=scale),
                                  reads=[spb], writes=[ptb])
                            kb.op("dve", lambda v: v.tensor_tensor(out=pt[:, 0:w], in0=pt[:, 0:w], in1=mask[:, 0:w],
                                                                   op=ALU.mult), reads=[bmask], writes=[ptb])
                            op_, opb = ops_.next()
                            fns = [lambda pe: pe.matmul(op_[:, 0:128], vts[di][:, b, :], pt[:, 0:128], start=True,
                                                        stop=not has_prev)]
                            if has_prev:
                                fns.append(lambda pe: pe.matmul(op_[:, 0:128], vts[di][:, b - 1, :], pt[:, 128:256],
                                                                start=False, stop=True))
                            fns.append(lambda pe: pe.matmul(op_[:, 128:256], ones[:], pt[:, 0:128], start=True,
                                                            stop=not has_prev))
                            if has_prev:
                                fns.append(lambda pe: pe.matmul(op_[:, 128:256], ones[:], pt[:, 128:256], start=False,
                                                                stop=True))
                            kb.mm(fns, reads=[bvs[di], bones, ptb], writes=[opb])
                            lc = slice(t0 - SPAN * sp, t0 - SPAN * sp + 127 * d + 1, d)
                            for a_ in range(2):
                                src = op_[:, a_ * 128:(a_ + 1) * 128]
                                if di == 0:
                                    kb.op("dve", lambda v: v.tensor_copy(out=acc[:, a_, lc], in_=src),
                                          reads=[opb], writes=[bacc])
                                else:
                                    kb.op("dve", lambda v: v.tensor_tensor(out=acc[:, a_, lc], in0=acc[:, a_, lc], in1=src,
                                                                           op=ALU.add), reads=[opb], writes=[bacc])
                ob, obb = obr.next()
                kb.op("dve", lambda v: v.reciprocal(out=acc[:, 1, :], in_=acc[:, 1, :]), writes=[bacc])
                kb.op("dve", lambda v: v.tensor_tensor(out=ob[:], in0=acc[:, 0, :], in1=acc[:, 1, :], op=ALU.mult),
                      reads=[bacc], writes=[obb])
                kb.dma("sp", aT[h * 128:(h + 1) * 128, sp * SPAN:(sp + 1) * SPAN], ob[:], reads=[obb], track=track)
        kb.finish(track)
    return nc


def _cols(v, n):
    return np.ascontiguousarray(np.asarray(v, np.float32).reshape(n, 128).T)


def _vd_layout(Vh):
    outs = []
    for d in DILS:
        nb = 128 // d
        a = Vh.reshape(nb, 128, d, 128).transpose(1, 2, 0, 3).reshape(128, d * nb, 128)
        outs.append(a)
    return np.stack(outs)


def run_l2(o1):
    nc = build_l2()
    in_maps = []
    for c in range(NCORES):
        rows = slice(c * 256, (c + 1) * 256)
        vd = np.stack([_vd_layout(np.ascontiguousarray(o1["vT"][(2 * c + hh) * 128:(2 * c + hh + 1) * 128, :].T))
                       for hh in range(2)])
        in_maps.append({"q": np.ascontiguousarray(o1["qT"][rows]), "k": np.ascontiguousarray(o1["kT"][rows]),
                        "Vd": np.ascontiguousarray(vd).reshape(2 * 3 * 128, S // 128, 128)})
    res = run_bass_kernel_spmd(nc, in_maps, core_ids=list(range(NCORES)))
    return np.concatenate([r["aT"] for r in res.results], axis=0)


def run_chain(first, inT, xTfull, w_out, w_gate, w_up, w_down, lnp, extra=None):
    nc = build_chain(first)
    in_maps = []
    for c in range(NTC):
        cs = slice(c * TOK, (c + 1) * TOK)
        m = {"inT": np.ascontiguousarray(inT[:, cs]), "xT": np.ascontiguousarray(xTfull[:, cs]),
             "w_out": w_out, "w_gate": w_gate, "w_up": w_up, "w_down": w_down, "lnp": lnp}
        if first:
            uT = extra["uT"]
            halo = uT[:, c * TOK - 2:c * TOK] if c > 0 else np.zeros((A_WIDTH, 2), np.float32)
            m.update(uTh=np.ascontiguousarray(np.concatenate([halo, uT[:, cs]], axis=1)),
                     gbT=np.ascontiguousarray(extra["gbT"][:, cs]), cwp=extra["cwp"],
                     tok0=np.full((128, 1), c * TOK, np.float32), w_inc=extra["w_inc"], w_uq=extra["w_uq"],
                     w_ukv=extra["w_ukv"], nrm=extra["nrm"])
        in_maps.append(m)
    res = run_bass_kernel_spmd(nc, in_maps, core_ids=list(range(NTC)))
    names = ["x2T"] + (["QnT", "QrT", "KnT", "VT", "KrT"] if first else [])
    return {n: np.concatenate([r[n] for r in res.results], axis=1) for n in names}


def run_l4(o3):
    nc = build_l4()
    in_maps = []
    for c in range(NCORES):
        vt = []
        for hh in range(HPC):
            h = c * HPC + hh
            Vh = np.ascontiguousarray(o3["VT"][h * 128:(h + 1) * 128, :].T)
            vt.append(Vh.reshape(S // 128, 128, 128).transpose(1, 0, 2))
        in_maps.append({"Qn": np.ascontiguousarray(o3["QnT"][c * 512:(c + 1) * 512]),
                        "Qr": np.ascontiguousarray(o3["QrT"][c * 256:(c + 1) * 256]),
                        "Kn": np.ascontiguousarray(o3["KnT"][c * 512:(c + 1) * 512]),
                        "Kr": np.ascontiguousarray(o3["KrT"]),
                        "Vt": np.ascontiguousarray(np.stack(vt)).reshape(HPC * 128, S // 128, 128)})
    res = run_bass_kernel_spmd(nc, in_maps, core_ids=list(range(NCORES)))
    return np.concatenate([r["OT"] for r in res.results], axis=0)


def kernel(x, w_in_a, conv_w, w_out_a, w_in_c, q_norm, kv_norm, w_uq, w_ukv, w_out_c,
           ln1_g, ln1_b, w_gate, w_up, w_down, ln2_g, ln2_b):
    f32 = lambda a: np.asarray(a, dtype=np.float32)
    xT = np.ascontiguousarray(f32(x).reshape(S, D).T)
    o1 = run_l1(xT, f32(w_in_a)[0])
    aT = run_l2(o1)
    ln1_g, ln1_b, ln2_g, ln2_b = f32(ln1_g), f32(ln1_b), f32(ln2_g), f32(ln2_b)
    lnp0 = np.ascontiguousarray(np.concatenate([_cols(ln1_g[0], KC), _cols(ln1_b[0], KC), _cols(ln2_g[0], KC),
                                                _cols(ln2_b[0], KC)], axis=1))
    lnp1 = np.ascontiguousarray(np.concatenate([_cols(ln1_g[1], KC), _cols(ln1_b[1], KC), _cols(ln2_g[1], KC),
                                                _cols(ln2_b[1], KC)], axis=1))
    cw = f32(conv_w)[0]
    extra = {"uT": o1["uT"], "gbT": o1["gbT"],
             "cwp": np.ascontiguousarray(cw.T.reshape(16, 128, 3).transpose(1, 0, 2).reshape(128, 48)),
             "w_inc": f32(w_in_c)[0], "w_uq": f32(w_uq)[0], "w_ukv": f32(w_ukv)[0],
             "nrm": np.ascontiguousarray(np.concatenate([_cols(f32(q_norm)[0], 12), _cols(f32(kv_norm)[0], 4)], axis=1))}
    w_gate, w_up, w_down = f32(w_gate), f32(w_up), f32(w_down)
    o3 = run_chain(True, aT, xT, f32(w_out_a)[0], w_gate[0], w_up[0], w_down[0], lnp0, extra)
    del o1, extra, aT
    OT = run_l4(o3)
    o5 = run_chain(False, OT, o3["x2T"], f32(w_out_c)[0], w_gate[1], w_up[1], w_down[1], lnp1)
    return np.ascontiguousarray(o5["x2T"].T).reshape(1, S, D).astype(np.float32)
```
